# Optimizing a Trainium2 kernel written in Bass

```python
import math
import jax, jax.numpy as jnp
from jax import lax
import numpy as np

D_MODEL = 2048
BATCH = 8
SEQ = 4096
DEPTH = 1

CHUNK = 64
Q_BLOCK = 128
HEAD_DIM = 128
N_HEADS_DIFF = D_MODEL // (2 * HEAD_DIM)
DIFF_QK_DIM = HEAD_DIM // 2
N_HEADS_SB = D_MODEL // (2 * HEAD_DIM)
DIFF_WIDTH = N_HEADS_DIFF * HEAD_DIM
SB_WIDTH = N_HEADS_SB * HEAD_DIM
MIX_WIDTH = DIFF_WIDTH + SB_WIDTH
IN_COLS = 3 * DIFF_WIDTH + 3 * SB_WIDTH
N_MEM = 256
N_HEADS_MEM = 4
MEM_HEAD_DIM = D_MODEL // N_HEADS_MEM
D_FF = ((8 * D_MODEL // 3 + 255) // 256) * 256
N_BUCKETS = 32
MAX_DISTANCE = 128
EPS = 1e-6

kernel_name = "hybrid_diff_stickbreak_chunk_encoder"


def rms_norm(x, g):
    xf = x.astype(jnp.float32)
    y = xf * lax.rsqrt(jnp.mean(xf * xf, axis=-1, keepdims=True) + EPS)
    return (y * g.astype(jnp.float32)).astype(x.dtype)


def lambda_init(layer_idx):
    return 0.8 - 0.6 * math.exp(-0.3 * layer_idx)


def t5_bucket(rel):
    nb = N_BUCKETS // 2
    ret = jnp.where(rel > 0, nb, 0)
    n = jnp.abs(rel)
    max_exact = nb // 2
    nf = jnp.maximum(n, 1).astype(jnp.float32)
    large = max_exact + (jnp.log(nf / max_exact) / math.log(MAX_DISTANCE / max_exact)
                         * (nb - max_exact)).astype(jnp.int32)
    large = jnp.minimum(large, nb - 1)
    return ret + jnp.where(n < max_exact, n, large)


def to_blocks(q):
    b, s, h, d = q.shape
    return q.reshape(b, s // Q_BLOCK, Q_BLOCK, h, d).transpose(1, 0, 2, 3, 4)


def from_blocks(o):
    nb, b, qb, h, d = o.shape
    return o.transpose(1, 0, 2, 3, 4).reshape(b, nb * qb, h, d)


def diff_attention(q, k, v, bias_table, lam, sub_gain, lam_init):
    seq = q.shape[1]
    key_pos = jnp.arange(seq, dtype=jnp.int32)
    k1, k2 = k[..., :DIFF_QK_DIM], k[..., DIFF_QK_DIM:]
    scale = DIFF_QK_DIM ** -0.5

    def block(args):
        i, qi = args
        q_pos = i * Q_BLOCK + jnp.arange(Q_BLOCK, dtype=jnp.int32)
        allowed = (key_pos[None, :] // CHUNK) <= (q_pos[:, None] // CHUNK)
        bias = bias_table[t5_bucket(key_pos[None, :] - q_pos[:, None])]
        bias = jnp.transpose(bias, (2, 0, 1)).astype(jnp.float32)[None]

        def softmax_map(qh, kh):
            s = jnp.einsum('bqhd,bkhd->bhqk', qh, kh).astype(jnp.float32) * scale + bias
            s = jnp.where(allowed, s, -jnp.inf)
            return jax.nn.softmax(s, axis=-1)

        p = softmax_map(qi[..., :DIFF_QK_DIM], k1) - lam * softmax_map(qi[..., DIFF_QK_DIM:], k2)
        return jnp.einsum('bhqk,bkhd->bqhd', p.astype(v.dtype), v)

    nb = seq // Q_BLOCK
    out = from_blocks(lax.map(block, (jnp.arange(nb, dtype=jnp.int32), to_blocks(q))))
    out = rms_norm(out, sub_gain) * (1.0 - lam_init)
    return out.reshape(out.shape[0], seq, -1)


def stick_breaking(q, k, v, out_gain):
    seq = q.shape[1]
    key_pos = jnp.arange(seq, dtype=jnp.int32)
    scale = HEAD_DIM ** -0.5

    def block(args):
        i, qi = args
        q_pos = i * Q_BLOCK + jnp.arange(Q_BLOCK, dtype=jnp.int32)
        strict = key_pos[None, :] < q_pos[:, None]
        z = jnp.einsum('bqhd,bkhd->bhqk', qi, k).astype(jnp.float32) * scale
        log_beta = jax.nn.log_sigmoid(z)
        log_keep = jnp.where(strict, jax.nn.log_sigmoid(-z), 0.0)
        after = lax.cumsum(log_keep, axis=log_keep.ndim - 1, reverse=True) - log_keep
        w = jnp.where(strict, jnp.exp(log_beta + after), 0.0)
        return jnp.einsum('bhqk,bkhd->bqhd', w.astype(v.dtype), v)

    nb = seq // Q_BLOCK
    out = from_blocks(lax.map(block, (jnp.arange(nb, dtype=jnp.int32), to_blocks(q))))
    out = rms_norm(out, out_gain)
    return out.reshape(out.shape[0], seq, -1)


def memory_attention(h, mem_n, w_q, w_kv, w_o):
    b, s, _ = h.shape
    q = (h @ w_q).reshape(b, s, N_HEADS_MEM, MEM_HEAD_DIM)
    kv = mem_n @ w_kv
    k = kv[..., :D_MODEL].reshape(b, N_MEM, N_HEADS_MEM, MEM_HEAD_DIM)
    v = kv[..., D_MODEL:].reshape(b, N_MEM, N_HEADS_MEM, MEM_HEAD_DIM)
    s_ = jnp.einsum('bqhd,bmhd->bhqm', q, k).astype(jnp.float32) * (MEM_HEAD_DIM ** -0.5)
    p = jax.nn.softmax(s_, axis=-1).astype(v.dtype)
    o = jnp.einsum('bhqm,bmhd->bqhd', p, v).reshape(b, s, D_MODEL)
    return o @ w_o


def setup_inputs(seed: int = 0) -> dict:
    key = jax.random.key(seed)
    ks = jax.random.split(key, 24)
    f32 = jnp.float32

    def w(k, shape, fan_in):
        return jax.random.normal(k, shape, f32) * (fan_in ** -0.5)

    def gain(k, shape):
        return 1.0 + 0.05 * jax.random.normal(k, shape, f32)

    return {
        "x": jax.random.normal(ks[0], (BATCH, SEQ, D_MODEL), f32),
        "mem": jax.random.normal(ks[1], (BATCH, N_MEM, D_MODEL), f32),
        "w_in": w(ks[2], (DEPTH, D_MODEL, IN_COLS), D_MODEL),
        "w_out": w(ks[3], (DEPTH, MIX_WIDTH, D_MODEL), MIX_WIDTH),
        "rel_bias": 0.5 * jax.random.normal(ks[4], (N_BUCKETS, N_HEADS_DIFF), f32),
        "lambda_q1": 0.1 * jax.random.normal(ks[5], (DEPTH, DIFF_QK_DIM), f32),
        "lambda_k1": 0.1 * jax.random.normal(ks[6], (DEPTH, DIFF_QK_DIM), f32),
        "lambda_q2": 0.1 * jax.random.normal(ks[7], (DEPTH, DIFF_QK_DIM), f32),
        "lambda_k2": 0.1 * jax.random.normal(ks[8], (DEPTH, DIFF_QK_DIM), f32),
        "diff_sub_gain": gain(ks[9], (DEPTH, HEAD_DIM)),
        "sb_gain": gain(ks[10], (DEPTH, HEAD_DIM)),
        "g_mix_pre": gain(ks[11], (DEPTH, D_MODEL)),
        "g_mix_post": gain(ks[12], (DEPTH, D_MODEL)),
        "w_mq": w(ks[13], (DEPTH, D_MODEL, D_MODEL), D_MODEL),
        "w_mkv": w(ks[14], (DEPTH, D_MODEL, 2 * D_MODEL), D_MODEL),
        "w_mo": w(ks[15], (DEPTH, D_MODEL, D_MODEL), D_MODEL),
        "g_mem_kv": gain(ks[16], (DEPTH, D_MODEL)),
        "g_mem_pre": gain(ks[17], (DEPTH, D_MODEL)),
        "g_mem_post": gain(ks[18], (DEPTH, D_MODEL)),
        "w_gate_up": w(ks[19], (DEPTH, D_MODEL, 2 * D_FF), D_MODEL),
        "w_down": w(ks[20], (DEPTH, D_FF, D_MODEL), D_FF),
        "g_ffn_pre": gain(ks[21], (DEPTH, D_MODEL)),
        "g_ffn_post": gain(ks[22], (DEPTH, D_MODEL)),
    }


def reference(x, mem, w_in, w_out, rel_bias, lambda_q1, lambda_k1, lambda_q2, lambda_k2,
              diff_sub_gain, sb_gain, g_mix_pre, g_mix_post, w_mq, w_mkv, w_mo,
              g_mem_kv, g_mem_pre, g_mem_post, w_gate_up, w_down, g_ffn_pre, g_ffn_post):
    b, s, _ = x.shape
    for l in range(DEPTH):
        lam_init = lambda_init(l)
        h = rms_norm(x, g_mix_pre[l])
        proj = h @ w_in[l]
        a_q, a_k, a_v, b_q, b_k, b_v = jnp.split(
            proj, [DIFF_WIDTH, 2 * DIFF_WIDTH, 3 * DIFF_WIDTH,
                   3 * DIFF_WIDTH + SB_WIDTH, 3 * DIFF_WIDTH + 2 * SB_WIDTH], axis=-1)
        a_q = a_q.reshape(b, s, N_HEADS_DIFF, HEAD_DIM)
        a_k = a_k.reshape(b, s, N_HEADS_DIFF, HEAD_DIM)
        a_v = a_v.reshape(b, s, N_HEADS_DIFF, HEAD_DIM)
        b_q = b_q.reshape(b, s, N_HEADS_SB, HEAD_DIM)
        b_k = b_k.reshape(b, s, N_HEADS_SB, HEAD_DIM)
        b_v = b_v.reshape(b, s, N_HEADS_SB, HEAD_DIM)
        lam = (jnp.exp(jnp.sum(lambda_q1[l].astype(jnp.float32) * lambda_k1[l].astype(jnp.float32)))
               - jnp.exp(jnp.sum(lambda_q2[l].astype(jnp.float32) * lambda_k2[l].astype(jnp.float32)))
               + lam_init)
        out_a = diff_attention(a_q, a_k, a_v, rel_bias, lam, diff_sub_gain[l], lam_init)
        out_b = stick_breaking(b_q, b_k, b_v, sb_gain[l])
        mix = jnp.concatenate([out_a, out_b], axis=-1) @ w_out[l]
        x = x + rms_norm(mix, g_mix_post[l])
        h = rms_norm(x, g_mem_pre[l])
        mem_n = rms_norm(mem, g_mem_kv[l])
        o = memory_attention(h, mem_n, w_mq[l], w_mkv[l], w_mo[l])
        x = x + rms_norm(o, g_mem_post[l])
        h = rms_norm(x, g_ffn_pre[l])
        gu = h @ w_gate_up[l]
        f = (jax.nn.silu(gu[..., :D_FF]) * gu[..., D_FF:]) @ w_down[l]
        x = x + rms_norm(f, g_ffn_post[l])
    return x
```

```python
import math
from contextlib import ExitStack

import numpy as np
import ml_dtypes
import concourse.bass as bass
import concourse.mybir as mybir
from concourse.bass_utils import run_bass_kernel_spmd

F32 = mybir.dt.float32
BF16 = mybir.dt.bfloat16
AF = mybir.ActivationFunctionType
ALU = mybir.AluOpType
AX = mybir.AxisListType

D = 2048
NKC = D // 128
DFF = 5632
NFC = DFF // 128
NMEM = 256
EPS = 1e-6
LAM_INIT = 0.8 - 0.6 * math.exp(0.0)
TT = 512
SELF_SYNC = True
DEBUG = False
DBG = {}


class Res:
    __slots__ = ("name", "w", "r")

    def __init__(self, name):
        self.name = name
        self.w = None
        self.r = []


class Eng:
    def __init__(self, name, e, sem, inorder=False):
        self.name = name
        self.e = e
        self.sem = sem
        self.cnt = 0
        self.known = {}
        self.inorder = inorder


class Slot:
    def __init__(self, sem):
        self.sem = sem
        self.cnt = 0


class Sched:
    def __init__(self, nc, es):
        self.nc = nc
        self.es = es
        mk = lambda n: es.enter_context(nc.semaphore(n))
        self.pe = Eng("pe", nc.tensor, mk("s_pe"), inorder=True)
        self.act = Eng("act", nc.scalar, mk("s_act"))
        self.dve = Eng("dve", nc.vector, mk("s_dve"))
        self.pool = Eng("pool", nc.gpsimd, mk("s_pool"))
        self.sp = Eng("sp", nc.sync, None)
        self.engs = [self.pe, self.act, self.dve, self.pool, self.sp]
        self.slots = []
        self.nsem = 4

    def slot(self, name):
        s = Slot(self.es.enter_context(self.nc.semaphore(name)))
        self.slots.append(s)
        self.nsem += 1
        return s

    def _wait(self, eng, ev, skip_sem=None):
        sem, val = ev
        if skip_sem is not None and sem is skip_sem:
            return
        if sem is eng.sem and (eng.inorder or not SELF_SYNC):
            return
        key = sem.num
        if eng.known.get(key, 0) >= val:
            return
        eng.e.wait_ge(sem, val)
        eng.known[key] = val

    def _deps(self, eng, reads, writes, skip_sem=None):
        for r in reads:
            if r.w is not None:
                self._wait(eng, r.w, skip_sem)
        for w in writes:
            if w.w is not None:
                self._wait(eng, w.w, skip_sem)
            for ev in w.r:
                self._wait(eng, ev, skip_sem)

    def _commit(self, ev, reads, writes):
        for r in reads:
            r.r.append(ev)
            if len(r.r) > 24:
                best = {}
                for s, v in r.r:
                    if s.num not in best or best[s.num][1] < v:
                        best[s.num] = (s, v)
                r.r = list(best.values())
        for w in writes:
            w.w = ev
            w.r = []

    def op(self, eng, fn, reads=(), writes=()):
        self._deps(eng, reads, writes)
        ins = fn()
        ins.then_inc(eng.sem, 1)
        eng.cnt += 1
        self._commit((eng.sem, eng.cnt), reads, writes)

    def mm(self, fns, reads=(), writes=()):
        eng = self.pe
        self._deps(eng, reads, writes)
        ins = None
        for f in fns:
            ins = f()
        ins.then_inc(eng.sem, 1)
        eng.cnt += 1
        self._commit((eng.sem, eng.cnt), reads, writes)

    def dma(self, q, out, in_, slot, reads=(), writes=()):
        self._deps(q, reads, writes, skip_sem=slot.sem)
        q.e.dma_start(out=out, in_=in_).then_inc(slot.sem, 16)
        slot.cnt += 16
        self._commit((slot.sem, slot.cnt), reads, writes)

    def barrier(self):
        for eng in self.engs:
            for o in self.engs:
                if o.sem is not None and o is not eng and o.cnt > 0:
                    self._wait(eng, (o.sem, o.cnt))
            for s in self.slots:
                if s.cnt > 0:
                    self._wait(eng, (s.sem, s.cnt))


def _t5_bucket(rel):
    nb = 16
    ret = np.where(rel > 0, nb, 0)
    n = np.abs(rel)
    max_exact = nb // 2
    nf = np.maximum(n, 1).astype(np.float32)
    large = max_exact + (np.log(nf / max_exact) / math.log(128 / max_exact) * (nb - max_exact)).astype(np.int32)
    large = np.minimum(large, nb - 1)
    return ret + np.where(n < max_exact, n, large)


def _consts():
    ident = np.eye(128, dtype=np.float32)
    j = np.arange(128)[:, None]
    s = np.arange(128)[None, :]
    tri = (j >= s).astype(np.float32)
    cb = np.concatenate([ident, tri], axis=1).astype(ml_dtypes.bfloat16)
    smask = (j < s).astype(np.float32)
    oh = np.zeros((33, 2, 128, 128), dtype=np.float32)
    q = np.arange(128)[:, None]
    k = np.arange(128)[None, :]
    for dl, off in ((0, 0), (1, -128)):
        rel = (k + off) - q
        b = _t5_bucket(rel)
        for bb in range(32):
            oh[bb, dl][b == bb] = 1.0
        if dl == 0:
            masked = (k // 64) > (q // 64)
            oh[32, dl][masked] = 1.0
    return cb, smask, oh.reshape(33, 2 * 128 * 128)


def build_nc(S):
    NT = S // TT
    NG = S // 512
    NB = S // 128
    nc = bass.Bass("TRN2", target_bir_lowering=False)

    def din(name, shape, dt=F32):
        return nc.dram_tensor(name, list(shape), dt, kind="ExternalInput").ap()

    x = din("x", [S, D])
    mem = din("mem", [NMEM, D])
    w_in = din("w_in", [D, 6144])
    w_out = din("w_out", [D, D])
    rel_bias = din("rel_bias", [32, 8])
    lq1 = din("lambda_q1", [1, 64]); lk1 = din("lambda_k1", [1, 64])
    lq2 = din("lambda_q2", [1, 64]); lk2 = din("lambda_k2", [1, 64])
    diff_gain = din("diff_sub_gain", [128, 1]); sb_gain = din("sb_gain", [128, 1])
    g_mix_pre = din("g_mix_pre", [128, NKC]); g_mix_post = din("g_mix_post", [1, D])
    w_mq = din("w_mq", [D, D]); w_mkv = din("w_mkv", [D, 2 * D]); w_mo = din("w_mo", [D, D])
    g_mem_kv = din("g_mem_kv", [128, NKC]); g_mem_pre = din("g_mem_pre", [128, NKC]); g_mem_post = din("g_mem_post", [1, D])
    w_gu = din("w_gate_up", [D, 2 * DFF]); w_down = din("w_down", [DFF, D])
    g_ffn_pre = din("g_ffn_pre", [128, NKC]); g_ffn_post = din("g_ffn_post", [1, D])
    c_b = din("c_b", [128, 256], BF16); c_smask = din("c_smask", [128, 128]); c_oh = din("c_oh", [33, 32768])
    y = nc.dram_tensor("y", [S, D], F32, kind="ExternalOutput").ap()
    if DEBUG:
        dbg1 = nc.dram_tensor("dbg1", [S, D], F32, kind="ExternalOutput").ap()
        dbg2 = nc.dram_tensor("dbg2", [S, D], F32, kind="ExternalOutput").ap()

    def dscr(name, shape, dt=BF16):
        if DEBUG and name in ("qkT", "vscr", "attnT"):
            return nc.dram_tensor(name, list(shape), dt, kind="ExternalOutput").ap()
        return nc.dram_tensor(name, list(shape), dt).ap()

    wb = {
        "w_in": dscr("w_in_b", [D, 6144]), "w_mkv": dscr("w_mkv_b", [D, 2 * D]), "w_out": dscr("w_out_b", [D, D]),
        "w_mq": dscr("w_mq_b", [D, D]), "w_mo": dscr("w_mo_b", [D, D]), "w_gu": dscr("w_gu_b", [D, 2 * DFF]),
        "w_down": dscr("w_down_b", [DFF, D]),
    }
    wsrc = {"w_in": w_in, "w_mkv": w_mkv, "w_out": w_out, "w_mq": w_mq, "w_mo": w_mo, "w_gu": w_gu, "w_down": w_down}
    qkT = dscr("qkT", [32, 128, S])
    vscr = dscr("vscr", [S, 2048])
    attnT = dscr("attnT", [16, 128, S])

    with ExitStack() as es:
        K = Sched(nc, es)
        pe, act, dve, pool, sp = K.pe, K.act, K.dve, K.pool, K.sp
        sb = lambda name, shape, dt: es.enter_context(nc.sbuf_tensor(name, list(shape), dt))

        wres = {}
        for name in ["w_in", "w_mkv", "w_out", "w_mq", "w_mo", "w_gu", "w_down"]:
            src, dst = wsrc[name], wb[name]
            rows, cols = src.shape
            r = Res(name)
            sl = K.slot("ws_" + name)
            step = 256
            for r0 in range(0, rows, step):
                K.dma(pool, dst[r0:r0 + step, :], src[r0:r0 + step, :], sl, writes=[r])
            wres[name] = r

        cb = sb("cb", [128, 256], BF16)
        ident = cb[:, 0:128]
        tri = cb[:, 128:256]
        onesb = sb("onesb", [128, 512], BF16)
        smask = sb("smask", [128, 128], F32)
        r_const = Res("const")
        sl_c = K.slot("sl_c")
        K.dma(sp, cb[:], c_b, sl_c, writes=[r_const])
        K.dma(sp, smask[:], c_smask, sl_c, writes=[r_const])
        small = sb("small", [128, 64 * 4 + 64], F32)
        gcols = sb("gcols", [128, 4 * NKC + 8], F32)
        r_small = Res("small")
        for i, v in enumerate([lq1, lk1, lq2, lk2]):
            K.dma(sp, small[:, i * 64:(i + 1) * 64], v.partition_broadcast(128), sl_c, writes=[r_small])
        for i, v in enumerate([g_mix_pre, g_mem_kv, g_mem_pre, g_ffn_pre]):
            K.dma(sp, gcols[:, i * NKC:(i + 1) * NKC], v, sl_c, writes=[r_small])
        K.dma(sp, gcols[:, 64:65], diff_gain, sl_c, writes=[r_small])
        K.dma(sp, gcols[:, 65:66], sb_gain, sl_c, writes=[r_small])
        c15b = sb("c15b", [128, 8], F32)
        K.dma(sp, c15b[:], rel_bias[15:16, :].partition_broadcast(128), sl_c, writes=[r_small])
        r_const.w = (sl_c.sem, sl_c.cnt)
        r_small.w = (sl_c.sem, sl_c.cnt)
        K.op(dve, lambda: nc.vector.memset(onesb[:], 1.0), writes=[r_const])
        sc0 = 256
        K.op(dve, lambda: nc.vector.tensor_tensor(out=small[:, 0:64], in0=small[:, 0:64], in1=small[:, 64:128], op=ALU.mult),
             reads=[r_small], writes=[r_small])
        K.op(dve, lambda: nc.vector.tensor_tensor(out=small[:, 128:192], in0=small[:, 128:192], in1=small[:, 192:256], op=ALU.mult),
             reads=[r_small], writes=[r_small])
        K.op(dve, lambda: nc.vector.reduce_sum(out=small[:, sc0:sc0 + 1], in_=small[:, 0:64], axis=AX.X), reads=[r_small], writes=[r_small])
        K.op(dve, lambda: nc.vector.reduce_sum(out=small[:, sc0 + 1:sc0 + 2], in_=small[:, 128:192], axis=AX.X), reads=[r_small], writes=[r_small])
        K.op(act, lambda: nc.scalar.activation(out=small[:, sc0 + 2:sc0 + 4], in_=small[:, sc0:sc0 + 2], func=AF.Exp), reads=[r_small], writes=[r_small])
        K.op(dve, lambda: nc.vector.tensor_tensor(out=small[:, sc0 + 4:sc0 + 5], in0=small[:, sc0 + 3:sc0 + 4], in1=small[:, sc0 + 2:sc0 + 3], op=ALU.subtract),
             reads=[r_small], writes=[r_small])
        K.op(dve, lambda: nc.vector.tensor_scalar(out=small[:, sc0 + 4:sc0 + 5], in0=small[:, sc0 + 4:sc0 + 5], scalar1=-LAM_INIT, scalar2=None, op0=ALU.add),
             reads=[r_small], writes=[r_small])
        nlam = small[:, sc0 + 4:sc0 + 5]
        K.op(dve, lambda: nc.vector.tensor_scalar(out=gcols[:, 64:65], in0=gcols[:, 64:65], scalar1=math.sqrt(128.0) * (1.0 - LAM_INIT), scalar2=None, op0=ALU.mult),
             reads=[r_small], writes=[r_small])
        K.op(dve, lambda: nc.vector.tensor_scalar(out=gcols[:, 65:66], in0=gcols[:, 65:66], scalar1=math.sqrt(128.0), scalar2=None, op0=ALU.mult),
             reads=[r_small], writes=[r_small])
        gdiff = gcols[:, 64:65]
        gsb = gcols[:, 65:66]
        stat = sb("stat", [128, 16], F32)
        r_stat = Res("stat")
        K.op(dve, lambda: nc.vector.memset(stat[:], 0.0), writes=[r_stat])
        nbias = sb("nbias", [128, 8], F32)
        r_nbias = Res("nbias")
        EB = sb("EB", [128, 2 * 128 * 8], F32)
        r_EB = Res("EB")

        def wview(name, k0, nk, c0, ncols):
            return wb[name].rearrange("(kc p) n -> p kc n", p=128)[:, k0:k0 + nk, c0:c0 + ncols]

        epsb = sb("epsb", [128, 2], F32)
        K.op(dve, lambda: nc.vector.memset(epsb[:, 0:1], EPS), writes=[r_const])
        K.op(dve, lambda: nc.vector.memset(epsb[:, 1:2], 128.0 * EPS), writes=[r_const])

        def rstd_from_ss(ss_ap, out_ap, n, reads, writes):
            K.op(act, lambda: nc.scalar.activation(out=out_ap, in_=ss_ap, func=AF.Ln, bias=epsb[:, 0:1], scale=1.0 / n),
                 reads=list(reads) + [r_const], writes=writes)
            K.op(act, lambda: nc.scalar.activation(out=out_ap, in_=out_ap, func=AF.Exp, scale=-0.5),
                 reads=writes, writes=writes)

        with ExitStack() as pa:
            sbA = lambda name, shape, dt: pa.enter_context(nc.sbuf_tensor(name, list(shape), dt))
            psA = lambda name, shape, dt: pa.enter_context(nc.psum_tensor(name, list(shape), dt))
            xin = [sbA(f"xin{i}", [128, 4, D], F32) for i in range(2)]
            r_xin = [Res(f"xin{i}") for i in range(2)]
            sl_xin = [K.slot(f"sl_xin{i}") for i in range(2)]
            junk = sbA("junkA", [128, D], BF16)
            r_junk = Res("junk")
            ssA = sbA("ssA", [128, 8], F32)
            r_ssA = [Res("ssA0"), Res("ssA1")]
            hb = [sbA(f"hbA{i}", [128, D], BF16) for i in range(2)]
            r_hb = [Res("hb0"), Res("hb1")]
            hT = [sbA(f"hTA{i}", [128, NKC, TT], BF16) for i in range(2)]
            r_hT = [Res("hT0"), Res("hT1")]
            NWS = 3
            ws = [sbA(f"wsA{i}", [128, NKC, 512], BF16) for i in range(NWS)]
            r_ws = [Res(f"wsA{i}") for i in range(NWS)]
            sl_ws = [K.slot(f"sl_wsA{i}") for i in range(NWS)]
            NST = 4
            stg = [sbA(f"stgA{i}", [128, 512], BF16) for i in range(NST)]
            r_stg = [Res(f"stg{i}") for i in range(NST)]
            sl_stg = [K.slot(f"sl_stgA{i}") for i in range(NST)]
            sqb = [sbA(f"sqbA{i}", [128, 512], BF16) for i in range(2)]
            r_sqb = [Res("sqb0"), Res("sqb1")]
            tmpm = sbA("tmpmA", [128, 2], F32)
            r_tmpm = [Res("tmpm0"), Res("tmpm1")]
            acc = [psA(f"accA{i}", [128, 512], F32) for i in range(3)]
            r_acc = [Res(f"accA{i}") for i in range(3)]
            tp = [psA(f"tpA{i}", [128, 8, 128], BF16) for i in range(2)]
            r_tp = [Res("tp0"), Res("tp1")]
            stp = psA("stpA", [128, 512], F32)
            r_stp = Res("stp")
            r_scr_qk = [[Res(f"qk{fc}_{t}") for t in range(NT)] for fc in range(32)]
            r_scr_v = [Res(f"v_{t}") for t in range(NT)]

            def load_x(t):
                b = t % 2
                K.dma(sp, xin[b][:], x[t * TT:(t + 1) * TT, :].rearrange("(j p) d -> p j d", p=128), sl_xin[b], writes=[r_xin[b]])

            cnt = {"w": 0, "acc": 0, "stg": 0, "sq": 0}
            wq = []

            def issue_w(c):
                i = cnt["w"] % NWS
                cnt["w"] += 1
                K.dma(sp, ws[i][:], wview("w_in", 0, NKC, c * 512, 512), sl_ws[i], reads=[wres["w_in"]], writes=[r_ws[i]])
                return i

            load_x(0)
            chunks = [(t, c) for t in range(NT) for c in range(12)]
            pend = []
            PRE = 2
            ci = 0
            for _ in range(PRE):
                if ci < len(chunks):
                    pend.append(issue_w(chunks[ci][1])); ci += 1
            for t in range(NT):
                b = t % 2
                if t + 1 < NT:
                    load_x(t + 1)
                for j in range(4):
                    hbj = j % 2
                    K.op(act, lambda: nc.scalar.activation(out=junk[:], in_=xin[b][:, j, :], func=AF.Square, accum_out=ssA[:, j:j + 1]),
                         reads=[r_xin[b]], writes=[r_junk, r_ssA[hbj]])
                    rstd_from_ss(ssA[:, j:j + 1], ssA[:, 4 + j:5 + j], D, [r_ssA[hbj]], [r_ssA[hbj]])
                    K.op(dve, lambda: nc.vector.tensor_scalar(out=hb[hbj][:], in0=xin[b][:, j, :], scalar1=ssA[:, 4 + j:5 + j], scalar2=None, op0=ALU.mult),
                         reads=[r_xin[b], r_ssA[hbj]], writes=[r_hb[hbj]])
                    for half in range(2):
                        K.mm([(lambda kc=kc: nc.tensor.transpose(out=tp[half][:, kc % 8, :], in_=hb[hbj][:, kc * 128:(kc + 1) * 128], identity=ident))
                              for kc in range(half * 8, half * 8 + 8)],
                             reads=[r_hb[hbj], r_const], writes=[r_tp[half]])
                        for kc in range(half * 8, half * 8 + 8):
                            if kc % 2 == 0:
                                K.op(act, lambda kc=kc: nc.scalar.activation(out=hT[b][:, kc, j * 128:(j + 1) * 128], in_=tp[half][:, kc % 8, :],
                                                                            func=AF.Copy, scale=gcols[:, kc:kc + 1]),
                                     reads=[r_tp[half], r_small], writes=[r_hT[b]])
                            else:
                                K.op(dve, lambda kc=kc: nc.vector.tensor_scalar(out=hT[b][:, kc, j * 128:(j + 1) * 128], in0=tp[half][:, kc % 8, :],
                                                                               scalar1=gcols[:, kc:kc + 1], scalar2=None, op0=ALU.mult),
                                     reads=[r_tp[half], r_small], writes=[r_hT[b]])
                for c in range(12):
                    wi = pend.pop(0)
                    if ci < len(chunks):
                        pend.append(issue_w(chunks[ci][1])); ci += 1
                    grp = c // 2
                    if grp in (2, 5):
                        vcol0 = (0 if grp == 2 else 1024) + (c % 2) * 512
                        for j in range(4):
                            a = cnt["acc"] % 3; cnt["acc"] += 1
                            K.mm([(lambda kc=kc: nc.tensor.matmul(acc[a][:], lhsT=hT[b][:, kc, j * 128:(j + 1) * 128], rhs=ws[wi][:, kc, :],
                                                                  start=(kc == 0), stop=(kc == NKC - 1))) for kc in range(NKC)],
                                 reads=[r_hT[b], r_ws[wi]], writes=[r_acc[a]])
                            s_ = cnt["stg"] % NST; cnt["stg"] += 1
                            K.op(act, lambda: nc.scalar.activation(out=stg[s_][:], in_=acc[a][:], func=AF.Copy),
                                 reads=[r_acc[a]], writes=[r_stg[s_]])
                            K.dma(sp, vscr[t * TT + j * 128:t * TT + (j + 1) * 128, vcol0:vcol0 + 512], stg[s_][:], sl_stg[s_],
                                  reads=[r_stg[s_]], writes=[r_scr_v[t]])
                    else:
                        fbase = {0: 0, 1: 8, 3: 16, 4: 24}[grp] + (c % 2) * 4
                        qscale = {0: 0.125, 1: 1.0, 3: 128.0 ** -0.5, 4: 1.0}[grp]
                        for fi in range(4):
                            fc = fbase + fi
                            a = cnt["acc"] % 3; cnt["acc"] += 1
                            K.mm([(lambda kc=kc: nc.tensor.matmul(acc[a][:], lhsT=ws[wi][:, kc, fi * 128:(fi + 1) * 128], rhs=hT[b][:, kc, :],
                                                                  start=(kc == 0), stop=(kc == NKC - 1))) for kc in range(NKC)],
                                 reads=[r_hT[b], r_ws[wi]], writes=[r_acc[a]])
                            s_ = cnt["stg"] % NST; cnt["stg"] += 1
                            K.op(act, lambda: nc.scalar.activation(out=stg[s_][:], in_=acc[a][:], func=AF.Copy, scale=qscale),
                                 reads=[r_acc[a]], writes=[r_stg[s_]])
                            if grp in (0, 1):
                                q_ = cnt["sq"] % 2; cnt["sq"] += 1
                                K.op(act, lambda: nc.scalar.activation(out=sqb[q_][:], in_=acc[a][:], func=AF.Square, scale=qscale),
                                     reads=[r_acc[a]], writes=[r_sqb[q_]])
                                K.mm([lambda: nc.tensor.matmul(stp[:], lhsT=onesb[:, 0:128], rhs=sqb[q_][:], start=True, stop=True)],
                                     reads=[r_sqb[q_], r_const], writes=[r_stp])
                                K.op(dve, lambda: nc.vector.reduce_max(out=tmpm[:, q_:q_ + 1], in_=stp[:], axis=AX.X),
                                     reads=[r_stp], writes=[r_tmpm[q_]])
                                col = fc
                                K.op(dve, lambda: nc.vector.tensor_tensor(out=stat[:, col:col + 1], in0=stat[:, col:col + 1], in1=tmpm[:, q_:q_ + 1], op=ALU.max),
                                     reads=[r_tmpm[q_], r_stat], writes=[r_stat])
                            K.dma(sp, qkT[fc, :, t * TT:(t + 1) * TT], stg[s_][:], sl_stg[s_], reads=[r_stg[s_]], writes=[r_scr_qk[fc][t]])
            K.barrier()

        K.op(dve, lambda: nc.vector.tensor_scalar(out=stat[:, 0:8], in0=stat[:, 0:8], scalar1=-4.2, scalar2=None, op0=ALU.mult),
             reads=[r_stat], writes=[r_stat])
        K.op(dve, lambda: nc.vector.scalar_tensor_tensor(out=nbias[:], in0=stat[:, 8:16], scalar=-1.0 / 15.0, in1=stat[:, 0:8], op0=ALU.mult, op1=ALU.add),
             reads=[r_stat], writes=[r_nbias])
        K.op(dve, lambda: nc.vector.tensor_tensor(out=nbias[:], in0=nbias[:], in1=c15b[:], op=ALU.add),
             reads=[r_small, r_nbias], writes=[r_nbias])

        with ExitStack() as pbias:
            sbB = lambda name, shape, dt: pbias.enter_context(nc.sbuf_tensor(name, list(shape), dt))
            psB = lambda name, shape, dt: pbias.enter_context(nc.psum_tensor(name, list(shape), dt))
            tab = sbB("tab", [33, 8], F32)
            r_tab = Res("tab")
            sl_b = K.slot("sl_b")
            K.op(dve, lambda: nc.vector.memset(tab[:], -30000.0), writes=[r_tab])
            K.dma(sp, tab[0:32, :], rel_bias, sl_b, writes=[r_tab])
            ohb = [sbB(f"ohb{i}", [33, 4096], F32) for i in range(2)]
            r_ohb = [Res("ohb0"), Res("ohb1")]
            sl_oh = [K.slot("sl_oh0"), K.slot("sl_oh1")]
            bps = [psB(f"bps{i}", [128, 512], F32) for i in range(2)]
            r_bps = [Res("bps0"), Res("bps1")]
            btmp = sbB("btmp", [128, 512], F32)
            r_btmp = Res("btmp")
            for pc in range(8):
                i = pc % 2
                K.dma(sp, ohb[i][:], c_oh[:, pc * 4096:(pc + 1) * 4096], sl_oh[i], writes=[r_ohb[i]])
                bi = pc % 2
                K.mm([(lambda qq=qq: nc.tensor.matmul(bps[bi][:, qq * 8:(qq + 1) * 8], lhsT=ohb[i][:, qq * 128:(qq + 1) * 128], rhs=tab[:],
                                                       start=True, stop=True)) for qq in range(32)],
                     reads=[r_ohb[i], r_tab], writes=[r_bps[bi]])
                K.op(dve, lambda: nc.vector.tensor_tensor(out=btmp[:, 0:256].rearrange("p (q h) -> p q h", h=8),
                                                          in0=bps[bi][:, 0:256].rearrange("p (q h) -> p q h", h=8),
                                                          in1=c15b[:].unsqueeze(1).to_broadcast([128, 32, 8]), op=ALU.subtract),
                     reads=[r_bps[bi], r_small], writes=[r_btmp])
                K.op(act, lambda: nc.scalar.activation(out=EB[:, pc * 256:(pc + 1) * 256], in_=btmp[:, 0:256], func=AF.Exp),
                     reads=[r_btmp], writes=[r_EB])
            K.barrier()
        EBv = EB[:].rearrange("p (dl q h) -> p dl q h", dl=2, q=128, h=8)

        with ExitStack() as pb:
            sbB = lambda name, shape, dt: pb.enter_context(nc.sbuf_tensor(name, list(shape), dt))
            psB = lambda name, shape, dt: pb.enter_context(nc.psum_tensor(name, list(shape), dt))
            QT = [sbB(f"QT{i}", [128, S], BF16) for i in range(4)]
            KT = [sbB(f"KT{i}", [128, S], BF16) for i in range(4)]
            VV = [sbB(f"VV{i}", [128, NB, 128], BF16) for i in range(4)]
            r_qkv = [Res(f"qkv{i}") for i in range(4)]
            sl_qkv = [K.slot(f"sl_qkv{i}") for i in range(4)]
            r_all_qk = [r for l in r_scr_qk for r in l]
            r_attn = [[Res(f"attn{h}_{g}") for g in range(NG)] for h in range(16)]
            NFIN = 2
            fin = [sbB(f"fin{i}", [128, 512], BF16) for i in range(NFIN)]
            r_fin = [Res(f"fin{i}") for i in range(NFIN)]
            sl_fin = [K.slot(f"sl_fin{i}") for i in range(NFIN)]
            fcnt = {"fin": 0}
            nrm = psB("nrmB", [128, 512], F32)
            r_nrm = Res("nrm")
            sqB = sbB("sqB", [128, 512], BF16)
            r_sqB = Res("sqB")
            rsB = sbB("rsB", [128, 512], F32)
            r_rsB = Res("rsB")
            osb = [sbB(f"osb{i}", [128, 512], F32) for i in range(2)]
            r_osb = [Res("osb0"), Res("osb1")]

            def load_head(slot_i, qfc, kfc, vcol):
                K.dma(sp, QT[slot_i][:], qkT[qfc], sl_qkv[slot_i], reads=r_all_qk[qfc * NT:(qfc + 1) * NT], writes=[r_qkv[slot_i]])
                K.dma(sp, KT[slot_i][:], qkT[kfc], sl_qkv[slot_i], reads=r_all_qk[kfc * NT:(kfc + 1) * NT], writes=[r_qkv[slot_i]])
                K.dma(sp, VV[slot_i][:], vscr[:, vcol:vcol + 128].rearrange("(kb p) d -> p kb d", p=128), sl_qkv[slot_i],
                      reads=r_scr_v, writes=[r_qkv[slot_i]])

            def head_norm_store(o_ap, r_o, gcol, hidx, g):
                K.op(act, lambda: nc.scalar.activation(out=sqB[:], in_=o_ap, func=AF.Square), reads=[r_o], writes=[r_sqB])
                K.mm([lambda: nc.tensor.matmul(nrm[:], lhsT=onesb[:, 0:128], rhs=sqB[:], start=True, stop=True)],
                     reads=[r_sqB, r_const], writes=[r_nrm])
                K.op(act, lambda: nc.scalar.activation(out=rsB[:], in_=nrm[:], func=AF.Ln, bias=epsb[:, 1:2], scale=1.0),
                     reads=[r_nrm, r_const], writes=[r_rsB])
                K.op(act, lambda: nc.scalar.activation(out=rsB[:], in_=rsB[:], func=AF.Exp, scale=-0.5),
                     reads=[r_rsB], writes=[r_rsB])
                f_ = fcnt["fin"] % NFIN; fcnt["fin"] += 1
                K.op(dve, lambda: nc.vector.scalar_tensor_tensor(out=fin[f_][:], in0=o_ap, scalar=gcol, in1=rsB[:], op0=ALU.mult, op1=ALU.mult),
                     reads=[r_o, r_rsB, r_small], writes=[r_fin[f_]])
                K.dma(sp, attnT[hidx, :, g * 512:(g + 1) * 512], fin[f_][:], sl_fin[f_], reads=[r_fin[f_]], writes=[r_attn[hidx][g]])

            with ExitStack() as pm:
                psM = lambda name, shape, dt: pm.enter_context(nc.psum_tensor(name, list(shape), dt))
                sbM = lambda name, shape, dt: pm.enter_context(nc.sbuf_tensor(name, list(shape), dt))
                Sps = [psM(f"Sps{m}", [128, 512], F32) for m in range(2)]
                r_S = [Res("S0"), Res("S1")]
                num = [psM(f"num{m}", [128, 512], F32) for m in range(2)]
                r_num = [Res("num0"), Res("num1")]
                zb2 = [psM(f"zb2_{i}", [128, 512], F32) for i in range(2)]; r_z = [Res("z0"), Res("z1")]
                aps = psM("aps", [128, 512], F32); r_a = Res("a")
                NE = 3
                Eb = [[sbM(f"E{m}_{i}", [128, 512], BF16) for i in range(NE)] for m in range(2)]
                r_E = [[Res(f"E{m}_{i}") for i in range(NE)] for m in range(2)]
                dacc = [[sbM(f"dacc{m}_{p}", [128, 512], F32) for p in range(2)] for m in range(2)]
                r_dacc = [[Res(f"dacc{m}_{p}") for p in range(2)] for m in range(2)]
                numS = [sbM(f"numS{m}", [128, 512], F32) for m in range(2)]
                r_numS = [Res("numS0"), Res("numS1")]
                post_ops = []
                chain_i = 0
                rr = [sbM(f"rr{m}", [128, 512], F32) for m in range(2)]
                r_rr = [Res("rr0"), Res("rr1")]
                ab = [sbM(f"ab{m}", [128, 512], F32) for m in range(2)]
                r_ab = [Res("ab0"), Res("ab1")]
                eb = [sbM(f"e_{i}", [128, 512], F32) for i in range(2)]
                r_e = [Res("e0"), Res("e1")]
                spb = [sbM(f"sp_{i}", [128, 512], BF16) for i in range(2)]
                r_sp = [Res("sp0"), Res("sp1")]
                ecb = sbM("ecb", [128, 512], F32); r_ec = Res("ec")
                wbf = [sbM(f"w_{i}", [128, 512], BF16) for i in range(2)]
                r_w = [Res("w0"), Res("w1")]
                racc = sbM("racc", [128, 512], BF16); r_racc = Res("racc")
                zb = sbM("zb", [128, 512], BF16)
                onesf = sbM("onesf", [128, 128], F32)
                K.op(dve, lambda: nc.vector.memset(zb[:], 0.0), writes=[r_const])
                ntri = sbM("ntri", [128, 128], BF16)
                nones = sbM("nones", [128, 128], BF16)
                K.op(dve, lambda: nc.vector.memset(nones[:], -1.0), writes=[r_const])
                K.op(dve, lambda: nc.vector.tensor_scalar(out=ntri[:], in0=tri, scalar1=-1.0, scalar2=None, op0=ALU.mult), reads=[r_const], writes=[r_const])
                K.op(dve, lambda: nc.vector.memset(onesf[:], 1.0), writes=[r_const])

                def load_pair(h):
                    load_head((h % 2) * 2, h, 8 + h, h * 128)
                    load_head((h % 2) * 2 + 1, 16 + h, 24 + h, 1024 + h * 128)

                load_pair(0)
                ecnt = 0
                for h in range(8):
                    sd = (h % 2) * 2
                    ss_ = sd + 1
                    if h + 1 < 8:
                        load_pair(h + 1)
                    for g in range(NG):
                        KS = 4 * g + 4
                        dp = chain_i % 2
                        chain_i += 1

                        def q0_of(kb):
                            i = kb - 4 * g
                            return (128 * i if i > 0 else 0), i

                        def d_S(k):
                            q0, _ = q0_of(k)
                            for m in range(2):
                                K.mm([lambda m=m: nc.tensor.matmul(Sps[m][:, q0:512], lhsT=KT[sd][64 * m:64 * m + 64, k * 128:(k + 1) * 128],
                                                                   rhs=QT[sd][64 * m:64 * m + 64, g * 512 + q0:(g + 1) * 512], start=True, stop=True)],
                                     reads=[r_qkv[sd]], writes=[r_S[m]])

                        def d_E(k, e_):
                            q0, _ = q0_of(k)
                            for m in range(2):
                                K.op(act, lambda m=m: nc.scalar.activation(out=Eb[m][e_][:, q0:512], in_=Sps[m][:, q0:512], func=AF.Exp,
                                                                          bias=nbias[:, h:h + 1], scale=1.0),
                                     reads=[r_S[m], r_nbias], writes=[r_E[m][e_]])
                                for jq in range(4):
                                    dl = (4 * g + jq) - k
                                    if dl in (0, 1) and jq * 128 >= q0:
                                        K.op(dve, lambda m=m, jq=jq, dl=dl: nc.vector.tensor_tensor(
                                            out=Eb[m][e_][:, jq * 128:(jq + 1) * 128], in0=Eb[m][e_][:, jq * 128:(jq + 1) * 128],
                                            in1=EBv[:, dl, :, h], op=ALU.mult),
                                            reads=[r_E[m][e_], r_EB], writes=[r_E[m][e_]])

                        def d_PV(k, e_):
                            q0, _ = q0_of(k)
                            for m in range(2):
                                K.mm([lambda m=m: nc.tensor.matmul(num[m][:, q0:512], lhsT=VV[sd][:, k, :], rhs=Eb[m][e_][:, q0:512],
                                                                   start=(k == 0), stop=(k == KS - 1))],
                                     reads=[r_E[m][e_], r_qkv[sd]], writes=[r_num[m]])
                            for m, (eng, ee) in enumerate(((pool, nc.gpsimd), (dve, nc.vector))):
                                if k == 0:
                                    K.op(eng, lambda m=m, ee=ee: ee.tensor_copy(out=dacc[m][dp][:], in_=Eb[m][e_][:]),
                                         reads=[r_E[m][e_]], writes=[r_dacc[m][dp]])
                                else:
                                    K.op(eng, lambda m=m, ee=ee: ee.tensor_tensor(out=dacc[m][dp][:, q0:512], in0=dacc[m][dp][:, q0:512], in1=Eb[m][e_][:, q0:512], op=ALU.add),
                                         reads=[r_E[m][e_], r_dacc[m][dp]], writes=[r_dacc[m][dp]])

                        def s_kb(k):
                            return 4 * g + 3 - k

                        def s_z(k):
                            kb = s_kb(k); q0, i = q0_of(kb); b_ = k % 2
                            K.mm([lambda: nc.tensor.matmul(zb2[b_][:, q0:512], lhsT=KT[ss_][:, kb * 128:(kb + 1) * 128],
                                                           rhs=QT[ss_][:, g * 512 + q0:(g + 1) * 512], start=True, stop=True)],
                                 reads=[r_qkv[ss_]], writes=[r_z[b_]])

                        def s_e(k):
                            kb = s_kb(k); q0, i = q0_of(kb); b_ = k % 2
                            K.op(act, lambda: nc.scalar.activation(out=eb[b_][:, q0:512], in_=zb2[b_][:, q0:512], func=AF.Exp),
                                 reads=[r_z[b_]], writes=[r_e[b_]])
                            if i >= 0:
                                K.op(dve, lambda: nc.vector.tensor_tensor(out=eb[b_][:, q0:q0 + 128], in0=eb[b_][:, q0:q0 + 128], in1=smask[:], op=ALU.mult),
                                     reads=[r_e[b_], r_const], writes=[r_e[b_]])

                        def s_SP(k):
                            kb = s_kb(k); q0, i = q0_of(kb); b_ = k % 2
                            K.op(act, lambda: nc.scalar.activation(out=spb[b_][:, q0:512], in_=eb[b_][:, q0:512], func=AF.Ln, bias=1.0, scale=1.0),
                                 reads=[r_e[b_]], writes=[r_sp[b_]])

                        def s_cum(k):
                            kb = s_kb(k); q0, i = q0_of(kb); b_ = k % 2
                            fns = [lambda: nc.tensor.matmul(zb2[b_][:, q0:512], lhsT=ntri[:], rhs=spb[b_][:, q0:512], start=False, stop=(k == 0),
                                                            skip_group_check=True)]
                            rd = [r_sp[b_], r_const]
                            if k > 0:
                                fns.append(lambda: nc.tensor.matmul(zb2[b_][:, q0:512], lhsT=nones[:], rhs=racc[:, q0:512], start=False, stop=True,
                                                                    skip_group_check=True))
                                rd.append(r_racc)
                            K.mm(fns, reads=rd, writes=[r_z[b_]])
                            if k + 1 < KS:
                                if k == 0:
                                    K.op(dve, lambda: nc.vector.tensor_copy(out=racc[:, q0:512], in_=spb[b_][:, q0:512]),
                                         reads=[r_sp[b_]], writes=[r_racc])
                                else:
                                    K.op(dve, lambda: nc.vector.tensor_tensor(out=racc[:, q0:512], in0=racc[:, q0:512], in1=spb[b_][:, q0:512], op=ALU.add),
                                         reads=[r_sp[b_], r_racc], writes=[r_racc])

                        def s_w(k):
                            kb = s_kb(k); q0, i = q0_of(kb); b_ = k % 2
                            K.op(act, lambda: nc.scalar.activation(out=wbf[b_][:, q0:512], in_=zb2[b_][:, q0:512], func=AF.Exp),
                                 reads=[r_z[b_]], writes=[r_w[b_]])
                            if i >= 0:
                                K.op(dve, lambda: nc.vector.tensor_tensor(out=wbf[b_][:, q0:q0 + 128], in0=wbf[b_][:, q0:q0 + 128], in1=smask[:], op=ALU.mult),
                                     reads=[r_w[b_], r_const], writes=[r_w[b_]])

                        def s_pv(k):
                            kb = s_kb(k); q0, i = q0_of(kb); b_ = k % 2
                            K.mm([lambda: nc.tensor.matmul(aps[:, q0:512], lhsT=VV[ss_][:, kb, :], rhs=wbf[b_][:, q0:512], start=False, stop=(k == KS - 1))],
                                 reads=[r_w[b_], r_qkv[ss_]], writes=[r_a])

                        K.op(dve, lambda: nc.vector.memset(racc[:, 0:384], 0.0), writes=[r_racc])
                        K.mm([lambda: nc.tensor.matmul(aps[:], lhsT=onesb[:, 0:128], rhs=zb[:], start=True, stop=False)],
                             reads=[r_const], writes=[r_a])
                        d_S(0)
                        s_z(0)
                        s_e(0)
                        for k in range(KS):
                            e_ = ecnt % NE; ecnt += 1
                            if k > 0:
                                s_w(k - 1)
                            s_SP(k)
                            s_cum(k)
                            if k + 1 < KS:
                                s_z(k + 1)
                            if k > 0:
                                s_pv(k - 1)
                            d_E(k, e_)
                            if k + 1 < KS:
                                d_S(k + 1)
                            d_PV(k, e_)
                            if k + 1 < KS:
                                s_e(k + 1)
                            if post_ops and k >= 1:
                                post_ops.pop(0)()
                        s_w(KS - 1)
                        s_pv(KS - 1)
                        while post_ops:
                            post_ops.pop(0)()
                        for m in range(2):
                            K.op(dve, lambda m=m: nc.vector.tensor_copy(out=numS[m][:], in_=num[m][:]), reads=[r_num[m]], writes=[r_numS[m]])
                        K.op(dve, lambda: nc.vector.tensor_copy(out=osb[1][:], in_=aps[:]), reads=[r_a], writes=[r_osb[1]])

                        def mk_post(dp=dp, h=h, g=g):
                            def den_rr(m):
                                K.mm([lambda: nc.tensor.matmul(nrm[:], lhsT=onesf[:], rhs=dacc[m][dp][:], start=True, stop=True)],
                                     reads=[r_dacc[m][dp], r_const], writes=[r_nrm])
                                K.op(dve, lambda: nc.vector.reciprocal(out=rr[m][:], in_=nrm[:]), reads=[r_nrm], writes=[r_rr[m]])
                                K.op(dve, lambda: nc.vector.tensor_tensor(out=ab[m][:], in0=numS[m][:], in1=rr[m][:], op=ALU.mult),
                                     reads=[r_numS[m], r_rr[m]], writes=[r_ab[m]])

                            def comb():
                                K.op(dve, lambda: nc.vector.scalar_tensor_tensor(out=osb[0][:], in0=ab[1][:], scalar=nlam, in1=ab[0][:], op0=ALU.mult, op1=ALU.add),
                                     reads=[r_ab[0], r_ab[1], r_small], writes=[r_osb[0]])
                                head_norm_store(osb[0][:], r_osb[0], gdiff, h, g)

                            return [lambda: den_rr(0), lambda: den_rr(1), comb,
                                    lambda: head_norm_store(osb[1][:], r_osb[1], gsb, 8 + h, g)]

                        post_ops.extend(mk_post())
                while post_ops:
                    post_ops.pop(0)()
                K.barrier()

        with ExitStack() as pc_:
            sbC = lambda name, shape, dt: pc_.enter_context(nc.sbuf_tensor(name, list(shape), dt))
            psC = lambda name, shape, dt: pc_.enter_context(nc.psum_tensor(name, list(shape), dt))
            xres = sbC("xres", [128, 4, D], F32); r_x = Res("xres"); sl_x = K.slot("sl_x")
            mix = sbC("mix", [128, 4, D], F32); r_mix = [Res(f"mix{j}") for j in range(4)]
            hbC = sbC("hbC", [128, D], BF16); r_hbC = Res("hbC")
            junkC = sbC("junkC", [128, D], BF16); r_junkC = Res("junkC")
            fmA = sbC("fmA", [128, NKC, TT], BF16); r_fmA = Res("fmA"); sl_fmA = K.slot("sl_fmA")
            RR = sbC("RR", [128, 24 * 512], BF16); r_RR = Res("RR")
            fmB = RR[:, 0:NKC * 512].rearrange("p (k t) -> p k t", t=512)
            PT = RR[:, NKC * 512:NKC * 512 + 4096].rearrange("p (h m t) -> p h m t", h=4, m=2)
            actT = RR[:].rearrange("p (k t) -> p k t", t=512)
            KmT = sbC("KmT", [128, NKC, NMEM], BF16); r_Km = Res("KmT")
            Vm = sbC("Vm", [128, 2, D], BF16); r_Vm = Res("Vm")
            gpost = sbC("gpost", [128, D], F32); r_gpost = Res("gpost"); sl_g = K.slot("sl_g")
            NWS = 3
            ws = [sbC(f"wsC{i}", [128, NKC, 512], BF16) for i in range(NWS)]
            r_ws = [Res(f"wsC{i}") for i in range(NWS)]
            sl_ws = [K.slot(f"sl_wsC{i}") for i in range(NWS)]
            ssC = sbC("ssC", [128, 16], F32); r_ssC = Res("ssC")
            smx = sbC("smx", [128, 16], F32); r_smx = [Res("smx0"), Res("smx1")]
            Pf = [sbC(f"Pf{i}", [128, NMEM], F32) for i in range(2)]; r_Pf = [Res("Pf0"), Res("Pf1")]
            Pn = [sbC(f"Pn{i}", [128, NMEM], BF16) for i in range(2)]; r_Pn = [Res("Pn0"), Res("Pn1")]
            sg = [sbC(f"sg{i}", [128, 512], F32) for i in range(2)]; r_sg = [Res("sg0"), Res("sg1")]
            NACC = 4
            acc = [psC(f"accC{i}", [128, 512], F32) for i in range(NACC)]
            r_acc = [Res(f"accC{i}") for i in range(NACC)]
            tpC = psC("tpC", [128, 8, 128], BF16); r_tp = Res("tpC")
            scp_ = [psC(f"scp{i}", [128, 512], F32) for i in range(2)]; r_sc = [Res("sc0"), Res("sc1")]
            ptp_ = psC("ptp", [128, 8, 128], BF16); r_ptp1 = Res("ptp"); r_ptp = [r_ptp1, r_ptp1]
            sl_y = K.slot("sl_y")
            r_y = Res("y")
            cnt = {"w": 0, "acc": 0}

            wplan = []

            def plan_tile():
                L = []
                for c in range(4): L.append([("w_out", 0, NKC, c * 512, 512, 0)])
                for c in range(4): L.append([("w_mq", 0, NKC, c * 512, 512, 0)])
                for c in range(4): L.append([("w_mo", 0, NKC, c * 512, 512, 0)])
                for hf, (c0, c1) in enumerate(((0, 12), (12, 22))):
                    for c2 in range(c0, c1):
                        L.append([("w_gu", 0, NKC, c2 * 256, 256, 0), ("w_gu", 0, NKC, DFF + c2 * 256, 256, 256)])
                    f0, f1 = c0 * 2, c1 * 2
                    nf = (f1 - f0) // 2
                    for c in range(4):
                        L.append([("w_down", f0, nf, c * 512, 512, 0)])
                        L.append([("w_down", f0 + nf, nf, c * 512, 512, 0)])
                return L

            for c in range(8): wplan.append([("w_mkv", 0, NKC, c * 512, 512, 0)])
            for t in range(NT): wplan.extend(plan_tile())
            wpos = {"i": 0}
            pend = []

            def issue_w():
                if wpos["i"] >= len(wplan):
                    return
                specs = wplan[wpos["i"]]; wpos["i"] += 1
                i = cnt["w"] % NWS; cnt["w"] += 1
                for (name, k0, nk, c0, ncols, d0) in specs:
                    K.dma(sp, ws[i][:, 0:nk, d0:d0 + ncols], wview(name, k0, nk, c0, ncols), sl_ws[i], reads=[wres[name]], writes=[r_ws[i]])
                pend.append(i)

            def next_w():
                wi = pend.pop(0)
                issue_w()
                return wi

            def nacc():
                a = cnt["acc"] % NACC; cnt["acc"] += 1
                return a

            for _ in range(2): issue_w()

            def prenorm_T(src_tile, nsub, r_src, gi, dstT, r_dst, ntok_sub=128):
                for j in range(nsub):
                    K.op(act, lambda: nc.scalar.activation(out=junkC[:], in_=src_tile[:, j, :], func=AF.Square, accum_out=ssC[:, j:j + 1]),
                         reads=[r_src], writes=[r_junkC, r_ssC])
                    rstd_from_ss(ssC[:, j:j + 1], ssC[:, 4 + j:5 + j], D, [r_ssC], [r_ssC])
                    K.op(dve, lambda: nc.vector.tensor_scalar(out=hbC[:], in0=src_tile[:, j, :], scalar1=ssC[:, 4 + j:5 + j], scalar2=None, op0=ALU.mult),
                         reads=[r_src, r_ssC], writes=[r_hbC])
                    for half in range(2):
                        K.mm([(lambda kc=kc: nc.tensor.transpose(out=tpC[:, kc % 8, :], in_=hbC[:, kc * 128:(kc + 1) * 128], identity=ident))
                              for kc in range(half * 8, half * 8 + 8)], reads=[r_hbC, r_const], writes=[r_tp])
                        for kc in range(half * 8, half * 8 + 8):
                            gc = gcols[:, gi * NKC + kc:gi * NKC + kc + 1]
                            if kc % 2 == 0:
                                K.op(act, lambda kc=kc, gc=gc: nc.scalar.activation(out=dstT[:, kc, j * 128:(j + 1) * 128], in_=tpC[:, kc % 8, :], func=AF.Copy, scale=gc),
                                     reads=[r_tp, r_small], writes=[r_dst])
                            else:
                                K.op(dve, lambda kc=kc, gc=gc: nc.vector.tensor_scalar(out=dstT[:, kc, j * 128:(j + 1) * 128], in0=tpC[:, kc % 8, :], scalar1=gc, scalar2=None, op0=ALU.mult),
                                     reads=[r_tp, r_small], writes=[r_dst])

            memx = mix[:, 0:2, :]
            sl_m = K.slot("sl_m")
            K.dma(pool, memx, mem.rearrange("(j p) d -> p j d", p=128), sl_m, writes=[r_mix[0]])
            prenorm_T(memx, 2, r_mix[0], 1, fmA, r_fmA)
            memT = fmA
            for c in range(8):
                wi = next_w()
                if c < 4:
                    for fi in range(4):
                        a = nacc()
                        K.mm([(lambda kc=kc: nc.tensor.matmul(acc[a][:, 0:NMEM], lhsT=ws[wi][:, kc, fi * 128:(fi + 1) * 128], rhs=memT[:, kc, 0:NMEM],
                                                              start=(kc == 0), stop=(kc == NKC - 1))) for kc in range(NKC)],
                             reads=[r_fmA, r_ws[wi]], writes=[r_acc[a]])
                        K.op(act, lambda: nc.scalar.activation(out=KmT[:, c * 4 + fi, :], in_=acc[a][:, 0:NMEM], func=AF.Copy, scale=512.0 ** -0.5),
                             reads=[r_acc[a]], writes=[r_Km])
                else:
                    for mc in range(2):
                        a = nacc()
                        K.mm([(lambda kc=kc: nc.tensor.matmul(acc[a][:], lhsT=memT[:, kc, mc * 128:(mc + 1) * 128], rhs=ws[wi][:, kc, :],
                                                              start=(kc == 0), stop=(kc == NKC - 1))) for kc in range(NKC)],
                             reads=[r_fmA, r_ws[wi]], writes=[r_acc[a]])
                        K.op(act, lambda: nc.scalar.activation(out=Vm[:, mc, (c - 4) * 512:(c - 3) * 512], in_=acc[a][:], func=AF.Copy),
                             reads=[r_acc[a]], writes=[r_Vm])

            def proj_tokmajor(srcT, r_srcT, extra_reads=()):
                for c in range(4):
                    wi = next_w()
                    for j in range(4):
                        a = nacc()
                        K.mm([(lambda kc=kc: nc.tensor.matmul(acc[a][:], lhsT=srcT[:, kc, j * 128:(j + 1) * 128], rhs=ws[wi][:, kc, :],
                                                              start=(kc == 0), stop=(kc == NKC - 1))) for kc in range(NKC)],
                             reads=[r_srcT, r_ws[wi]] + list(extra_reads), writes=[r_acc[a]])
                        if j % 2 == 0:
                            K.op(act, lambda: nc.scalar.activation(out=mix[:, j, c * 512:(c + 1) * 512], in_=acc[a][:], func=AF.Copy),
                                 reads=[r_acc[a]], writes=[r_mix[j]])
                        else:
                            K.op(dve, lambda: nc.vector.tensor_copy(out=mix[:, j, c * 512:(c + 1) * 512], in_=acc[a][:]),
                                 reads=[r_acc[a]], writes=[r_mix[j]])

            def postnorm_residual(gvec):
                K.dma(pool, gpost[:], gvec.partition_broadcast(128), sl_g, writes=[r_gpost])
                for j in range(4):
                    K.op(act, lambda: nc.scalar.activation(out=junkC[:], in_=mix[:, j, :], func=AF.Square, accum_out=ssC[:, 8 + j:9 + j]),
                         reads=[r_mix[j]], writes=[r_junkC, r_ssC])
                    rstd_from_ss(ssC[:, 8 + j:9 + j], ssC[:, 12 + j:13 + j], D, [r_ssC], [r_ssC])
                    K.op(dve, lambda: nc.vector.scalar_tensor_tensor(out=mix[:, j, :], in0=mix[:, j, :], scalar=ssC[:, 12 + j:13 + j], in1=gpost[:], op0=ALU.mult, op1=ALU.mult),
                         reads=[r_mix[j], r_ssC, r_gpost], writes=[r_mix[j]])
                    K.op(pool, lambda: nc.gpsimd.tensor_tensor(out=xres[:, j, :], in0=xres[:, j, :], in1=mix[:, j, :], op=ALU.add),
                         reads=[r_mix[j], r_x], writes=[r_x])

            r_attn_all = [r for l in r_attn for r in l]
            for t in range(NT):
                tok = slice(t * TT, (t + 1) * TT)
                K.dma(pool, xres[:], x[tok, :].rearrange("(j p) d -> p j d", p=128), sl_x, reads=[r_y], writes=[r_x])
                K.dma(sp, fmA[:], attnT[:, :, tok].rearrange("k p t -> p k t"), sl_fmA, reads=r_attn_all if t == 0 else [], writes=[r_fmA])
                proj_tokmajor(fmA, r_fmA)
                postnorm_residual(g_mix_post)
                if DEBUG:
                    K.dma(pool, dbg1[tok, :].rearrange("(j p) d -> p j d", p=128), xres[:], sl_y, reads=[r_x], writes=[r_y])
                prenorm_T(xres, 4, r_x, 2, fmB, r_RR)
                for c in range(4):
                    wi = next_w()
                    for fi in range(4):
                        a = nacc()
                        K.mm([(lambda kc=kc: nc.tensor.matmul(acc[a][:], lhsT=ws[wi][:, kc, fi * 128:(fi + 1) * 128], rhs=fmB[:, kc, :],
                                                              start=(kc == 0), stop=(kc == NKC - 1))) for kc in range(NKC)],
                             reads=[r_RR, r_ws[wi]], writes=[r_acc[a]])
                        K.op(act, lambda: nc.scalar.activation(out=fmA[:, c * 4 + fi, :], in_=acc[a][:], func=AF.Copy),
                             reads=[r_acc[a]], writes=[r_fmA])
                qT = fmA
                for j in range(4):
                    for hd in range(4):
                        u = hd % 2
                        K.mm([(lambda dc=dc: nc.tensor.matmul(scp_[u][:, 0:NMEM], lhsT=qT[:, 4 * hd + dc, j * 128:(j + 1) * 128], rhs=KmT[:, 4 * hd + dc, :],
                                                              start=(dc == 0), stop=(dc == 3))) for dc in range(4)],
                             reads=[r_fmA, r_Km], writes=[r_sc[u]])
                        K.op(dve, lambda: nc.vector.reduce_max(out=smx[:, 4 * u:4 * u + 1], in_=scp_[u][:, 0:NMEM], axis=AX.X), reads=[r_sc[u]], writes=[r_smx[u]])
                        K.op(dve, lambda: nc.vector.tensor_scalar(out=smx[:, 4 * u + 1:4 * u + 2], in0=smx[:, 4 * u:4 * u + 1], scalar1=-1.0, scalar2=None, op0=ALU.mult),
                             reads=[r_smx[u]], writes=[r_smx[u]])
                        K.op(act, lambda: nc.scalar.activation(out=Pf[u][:], in_=scp_[u][:, 0:NMEM], func=AF.Exp, bias=smx[:, 4 * u + 1:4 * u + 2], scale=1.0,
                                                               accum_out=smx[:, 4 * u + 2:4 * u + 3]),
                             reads=[r_sc[u], r_smx[u]], writes=[r_Pf[u], r_smx[u]])
                        K.op(dve, lambda: nc.vector.reciprocal(out=smx[:, 4 * u + 3:4 * u + 4], in_=smx[:, 4 * u + 2:4 * u + 3]), reads=[r_smx[u]], writes=[r_smx[u]])
                        K.op(dve, lambda: nc.vector.tensor_scalar(out=Pn[u][:], in0=Pf[u][:], scalar1=smx[:, 4 * u + 3:4 * u + 4], scalar2=None, op0=ALU.mult),
                             reads=[r_Pf[u], r_smx[u]], writes=[r_Pn[u]])
                        K.mm([(lambda mc=mc: nc.tensor.transpose(out=ptp_[:, mc, :], in_=Pn[u][:, mc * 128:(mc + 1) * 128], identity=ident)) for mc in range(2)],
                             reads=[r_Pn[u], r_const], writes=[r_ptp[u]])
                        K.op(act, lambda: nc.scalar.activation(out=PT[:, hd, :, j * 128:(j + 1) * 128], in_=ptp_[:, 0:2, :], func=AF.Copy),
                             reads=[r_ptp[u]], writes=[r_RR])
                for hd in range(4):
                    for dc in range(4):
                        a = nacc()
                        K.mm([(lambda mc=mc: nc.tensor.matmul(acc[a][:], lhsT=Vm[:, mc, hd * 512 + dc * 128:hd * 512 + (dc + 1) * 128], rhs=PT[:, hd, mc, :],
                                                              start=(mc == 0), stop=(mc == 1))) for mc in range(2)],
                             reads=[r_RR, r_Vm], writes=[r_acc[a]])
                        K.op(act, lambda: nc.scalar.activation(out=fmB[:, 4 * hd + dc, :], in_=acc[a][:], func=AF.Copy),
                             reads=[r_acc[a]], writes=[r_RR])
                proj_tokmajor(fmB, r_RR)
                postnorm_residual(g_mem_post)
                if DEBUG:
                    K.dma(pool, dbg2[tok, :].rearrange("(j p) d -> p j d", p=128), xres[:], sl_y, reads=[r_x], writes=[r_y])
                prenorm_T(xres, 4, r_x, 3, fmA, r_fmA)
                h3T = fmA
                for hf, (c0, c1) in enumerate(((0, 12), (12, 22))):
                    for c2 in range(c0, c1):
                        wi = next_w()
                        for fi in range(2):
                            ag = nacc()
                            K.mm([(lambda kc=kc: nc.tensor.matmul(acc[ag][:], lhsT=ws[wi][:, kc, fi * 128:(fi + 1) * 128], rhs=h3T[:, kc, :],
                                                                  start=(kc == 0), stop=(kc == NKC - 1))) for kc in range(NKC)],
                                 reads=[r_fmA, r_ws[wi]], writes=[r_acc[ag]])
                            au = nacc()
                            K.mm([(lambda kc=kc: nc.tensor.matmul(acc[au][:], lhsT=ws[wi][:, kc, 256 + fi * 128:256 + (fi + 1) * 128], rhs=h3T[:, kc, :],
                                                                  start=(kc == 0), stop=(kc == NKC - 1))) for kc in range(NKC)],
                                 reads=[r_fmA, r_ws[wi]], writes=[r_acc[au]])
                            s_ = fi % 2
                            K.op(act, lambda: nc.scalar.activation(out=sg[s_][:], in_=acc[ag][:], func=AF.Silu), reads=[r_acc[ag]], writes=[r_sg[s_]])
                            K.op(dve, lambda: nc.vector.tensor_tensor(out=actT[:, (c2 - c0) * 2 + fi, :], in0=sg[s_][:], in1=acc[au][:], op=ALU.mult),
                                 reads=[r_sg[s_], r_acc[au]], writes=[r_RR])
                    f0, f1 = c0 * 2, c1 * 2
                    nf = (f1 - f0) // 2
                    for c in range(4):
                        aj = [nacc() for _ in range(4)]
                        for pi in range(2):
                            wi = next_w()
                            for j in range(4):
                                a = aj[j]
                                K.mm([(lambda kk=kk: nc.tensor.matmul(acc[a][:], lhsT=actT[:, pi * nf + kk, j * 128:(j + 1) * 128], rhs=ws[wi][:, kk, :],
                                                                      start=(pi == 0 and kk == 0), stop=(pi == 1 and kk == nf - 1))) for kk in range(nf)],
                                     reads=[r_RR, r_ws[wi]], writes=[r_acc[a]])
                        for j in range(4):
                            a = aj[j]
                            if hf == 0:
                                K.op(act, lambda: nc.scalar.activation(out=mix[:, j, c * 512:(c + 1) * 512], in_=acc[a][:], func=AF.Copy),
                                     reads=[r_acc[a]], writes=[r_mix[j]])
                            else:
                                K.op(dve, lambda: nc.vector.tensor_tensor(out=mix[:, j, c * 512:(c + 1) * 512], in0=mix[:, j, c * 512:(c + 1) * 512], in1=acc[a][:], op=ALU.add),
                                     reads=[r_acc[a], r_mix[j]], writes=[r_mix[j]])
                postnorm_residual(g_ffn_post)
                K.dma(pool, y[tok, :].rearrange("(j p) d -> p j d", p=128), xres[:], sl_y, reads=[r_x], writes=[r_y])
            K.barrier()
    return nc


_CACHE = {}


def _prep_shared(inputs):
    cbm, smask, oh = _consts()
    f = lambda a: np.ascontiguousarray(np.asarray(a, dtype=np.float32))

    def gcol(v):
        return np.ascontiguousarray(f(v).reshape(NKC, 128).T)

    sh = {
        "w_in": f(inputs["w_in"])[0], "w_out": f(inputs["w_out"])[0], "rel_bias": f(inputs["rel_bias"]),
        "lambda_q1": f(inputs["lambda_q1"]), "lambda_k1": f(inputs["lambda_k1"]),
        "lambda_q2": f(inputs["lambda_q2"]), "lambda_k2": f(inputs["lambda_k2"]),
        "diff_sub_gain": f(inputs["diff_sub_gain"]).reshape(128, 1), "sb_gain": f(inputs["sb_gain"]).reshape(128, 1),
        "g_mix_pre": gcol(inputs["g_mix_pre"]), "g_mix_post": f(inputs["g_mix_post"]),
        "w_mq": f(inputs["w_mq"])[0], "w_mkv": f(inputs["w_mkv"])[0], "w_mo": f(inputs["w_mo"])[0],
        "g_mem_kv": gcol(inputs["g_mem_kv"]), "g_mem_pre": gcol(inputs["g_mem_pre"]), "g_mem_post": f(inputs["g_mem_post"]),
        "w_gate_up": f(inputs["w_gate_up"])[0], "w_down": f(inputs["w_down"])[0],
        "g_ffn_pre": gcol(inputs["g_ffn_pre"]), "g_ffn_post": f(inputs["g_ffn_post"]),
        "c_b": cbm, "c_smask": smask, "c_oh": oh,
    }
    return sh


def kernel(**inputs):
    x = np.asarray(inputs["x"], dtype=np.float32)
    mem = np.asarray(inputs["mem"], dtype=np.float32)
    B, S, _ = x.shape
    if S not in _CACHE:
        _CACHE[S] = build_nc(S)
    nc = _CACHE[S]
    sh = _prep_shared(inputs)
    in_maps = []
    for b in range(B):
        m = dict(sh)
        m["x"] = np.ascontiguousarray(x[b])
        m["mem"] = np.ascontiguousarray(mem[b])
        in_maps.append(m)
    res = run_bass_kernel_spmd(nc, in_maps, core_ids=list(range(B)))
    if DEBUG:
        for k in ("qkT", "vscr", "attnT", "dbg1", "dbg2"):
            if k in res.results[0]:
                DBG[k] = np.asarray(res.results[0][k])
    return np.stack([np.asarray(r["y"], dtype=np.float32) for r in res.results], axis=0)
```

```python
import math
from contextlib import ExitStack

import numpy as np
import ml_dtypes
import concourse.bass as bass
import concourse.mybir as mybir
from concourse.bass_utils import run_bass_kernel_spmd

F32 = mybir.dt.float32
BF16 = mybir.dt.bfloat16
AF = mybir.ActivationFunctionType
ALU = mybir.AluOpType
AX = mybir.AxisListType

D = 2048
NKC = D // 128
DFF = 5632
NFC = DFF // 128
NMEM = 256
EPS = 1e-6
LAM_INIT = 0.8 - 0.6 * math.exp(0.0)
TT = 512
SELF_SYNC = True
DEBUG = False
DBG = {}


class Res:
    __slots__ = ("name", "w", "r")

    def __init__(self, name):
        self.name = name
        self.w = None
        self.r = []


class Eng:
    def __init__(self, name, e, sem, inorder=False):
        self.name = name
        self.e = e
        self.sem = sem
        self.cnt = 0
        self.known = {}
        self.inorder = inorder


class Slot:
    def __init__(self, sem):
        self.sem = sem
        self.cnt = 0


class Sched:
    def __init__(self, nc, es):
        self.nc = nc
        self.es = es
        mk = lambda n: es.enter_context(nc.semaphore(n))
        self.pe = Eng("pe", nc.tensor, mk("s_pe"), inorder=True)
        self.act = Eng("act", nc.scalar, mk("s_act"))
        self.dve = Eng("dve", nc.vector, mk("s_dve"))
        self.pool = Eng("pool", nc.gpsimd, mk("s_pool"))
        self.sp = Eng("sp", nc.sync, None)
        self.engs = [self.pe, self.act, self.dve, self.pool, self.sp]
        self.slots = []
        self.nsem = 4

    def slot(self, name):
        s = Slot(self.es.enter_context(self.nc.semaphore(name)))
        self.slots.append(s)
        self.nsem += 1
        return s

    def _wait(self, eng, ev, skip_sem=None):
        sem, val = ev
        if skip_sem is not None and sem is skip_sem:
            return
        if sem is eng.sem and (eng.inorder or not SELF_SYNC):
            return
        key = sem.num
        if eng.known.get(key, 0) >= val:
            return
        eng.e.wait_ge(sem, val)
        eng.known[key] = val

    def _deps(self, eng, reads, writes, skip_sem=None):
        for r in reads:
            if r.w is not None:
                self._wait(eng, r.w, skip_sem)
        for w in writes:
            if w.w is not None:
                self._wait(eng, w.w, skip_sem)
            for ev in w.r:
                self._wait(eng, ev, skip_sem)

    def _commit(self, ev, reads, writes):
        for r in reads:
            r.r.append(ev)
            if len(r.r) > 24:
                best = {}
                for s, v in r.r:
                    if s.num not in best or best[s.num][1] < v:
                        best[s.num] = (s, v)
                r.r = list(best.values())
        for w in writes:
            w.w = ev
            w.r = []

    def op(self, eng, fn, reads=(), writes=()):
        self._deps(eng, reads, writes)
        ins = fn()
        ins.then_inc(eng.sem, 1)
        eng.cnt += 1
        self._commit((eng.sem, eng.cnt), reads, writes)

    def mm(self, fns, reads=(), writes=()):
        eng = self.pe
        self._deps(eng, reads, writes)
        ins = None
        for f in fns:
            ins = f()
        ins.then_inc(eng.sem, 1)
        eng.cnt += 1
        self._commit((eng.sem, eng.cnt), reads, writes)

    def dma(self, q, out, in_, slot, reads=(), writes=()):
        self._deps(q, reads, writes, skip_sem=slot.sem)
        q.e.dma_start(out=out, in_=in_).then_inc(slot.sem, 16)
        slot.cnt += 16
        self._commit((slot.sem, slot.cnt), reads, writes)

    def barrier(self):
        for eng in self.engs:
            for o in self.engs:
                if o.sem is not None and o is not eng and o.cnt > 0:
                    self._wait(eng, (o.sem, o.cnt))
            for s in self.slots:
                if s.cnt > 0:
                    self._wait(eng, (s.sem, s.cnt))


def _t5_bucket(rel):
    nb = 16
    ret = np.where(rel > 0, nb, 0)
    n = np.abs(rel)
    max_exact = nb // 2
    nf = np.maximum(n, 1).astype(np.float32)
    large = max_exact + (np.log(nf / max_exact) / math.log(128 / max_exact) * (nb - max_exact)).astype(np.int32)
    large = np.minimum(large, nb - 1)
    return ret + np.where(n < max_exact, n, large)


def _consts():
    ident = np.eye(128, dtype=np.float32)
    j = np.arange(128)[:, None]
    s = np.arange(128)[None, :]
    tri = (j >= s).astype(np.float32)
    cb = np.concatenate([ident, tri], axis=1).astype(ml_dtypes.bfloat16)
    smask = (j < s).astype(np.float32)
    oh = np.zeros((33, 2, 128, 128), dtype=np.float32)
    q = np.arange(128)[:, None]
    k = np.arange(128)[None, :]
    for dl, off in ((0, 0), (1, -128)):
        rel = (k + off) - q
        b = _t5_bucket(rel)
        for bb in range(32):
            oh[bb, dl][b == bb] = 1.0
        if dl == 0:
            masked = (k // 64) > (q // 64)
            oh[32, dl][masked] = 1.0
    return cb, smask, oh.reshape(33, 2 * 128 * 128)


def build_nc(S):
    NT = S // TT
    NG = S // 512
    NB = S // 128
    nc = bass.Bass("TRN2", target_bir_lowering=False)

    def din(name, shape, dt=F32):
        return nc.dram_tensor(name, list(shape), dt, kind="ExternalInput").ap()

    x = din("x", [S, D])
    mem = din("mem", [NMEM, D])
    w_in = din("w_in", [D, 6144])
    w_out = din("w_out", [D, D])
    rel_bias = din("rel_bias", [32, 8])
    lq1 = din("lambda_q1", [1, 64]); lk1 = din("lambda_k1", [1, 64])
    lq2 = din("lambda_q2", [1, 64]); lk2 = din("lambda_k2", [1, 64])
    diff_gain = din("diff_sub_gain", [128, 1]); sb_gain = din("sb_gain", [128, 1])
    g_mix_pre = din("g_mix_pre", [128, NKC]); g_mix_post = din("g_mix_post", [1, D])
    w_mq = din("w_mq", [D, D]); w_mkv = din("w_mkv", [D, 2 * D]); w_mo = din("w_mo", [D, D])
    g_mem_kv = din("g_mem_kv", [128, NKC]); g_mem_pre = din("g_mem_pre", [128, NKC]); g_mem_post = din("g_mem_post", [1, D])
    w_gu = din("w_gate_up", [D, 2 * DFF]); w_down = din("w_down", [DFF, D])
    g_ffn_pre = din("g_ffn_pre", [128, NKC]); g_ffn_post = din("g_ffn_post", [1, D])
    gv_mix_pre = din("gv_mix_pre", [1, D]); gv_mem_kv = din("gv_mem_kv", [1, D])
    gv_mem_pre = din("gv_mem_pre", [1, D]); gv_ffn_pre = din("gv_ffn_pre", [1, D])
    c_b = din("c_b", [128, 256], BF16); c_smask = din("c_smask", [128, 128]); c_oh = din("c_oh", [33, 32768])
    y = nc.dram_tensor("y", [S, D], F32, kind="ExternalOutput").ap()
    if DEBUG:
        dbg1 = nc.dram_tensor("dbg1", [S, D], F32, kind="ExternalOutput").ap()
        dbg2 = nc.dram_tensor("dbg2", [S, D], F32, kind="ExternalOutput").ap()

    def dscr(name, shape, dt=BF16):
        if DEBUG and name in ("qkT", "vscr", "attnT"):
            return nc.dram_tensor(name, list(shape), dt, kind="ExternalOutput").ap()
        return nc.dram_tensor(name, list(shape), dt).ap()

    wb = {
        "w_in": dscr("w_in_b", [D, 6144]), "w_mkv": dscr("w_mkv_b", [D, 2 * D]), "w_out": dscr("w_out_b", [D, D]),
        "w_mq": dscr("w_mq_b", [D, D]), "w_mo": dscr("w_mo_b", [D, D]), "w_gu": dscr("w_gu_b", [D, 2 * DFF]),
        "w_down": dscr("w_down_b", [DFF, D]),
    }
    wsrc = {"w_in": w_in, "w_mkv": w_mkv, "w_out": w_out, "w_mq": w_mq, "w_mo": w_mo, "w_gu": w_gu, "w_down": w_down}
    qkT = dscr("qkT", [32, 128, S])
    vscr = dscr("vscr", [S, 2048])
    attnT = dscr("attnT", [16, 128, S])

    with ExitStack() as es:
        K = Sched(nc, es)
        pe, act, dve, pool, sp = K.pe, K.act, K.dve, K.pool, K.sp
        sb = lambda name, shape, dt: es.enter_context(nc.sbuf_tensor(name, list(shape), dt))

        wres = {}
        for name in ["w_in", "w_mkv", "w_out", "w_mq", "w_mo", "w_gu", "w_down"]:
            src, dst = wsrc[name], wb[name]
            rows, cols = src.shape
            r = Res(name)
            sl = K.slot("ws_" + name)
            step = 256
            for r0 in range(0, rows, step):
                K.dma(pool, dst[r0:r0 + step, :], src[r0:r0 + step, :], sl, writes=[r])
            wres[name] = r

        cb = sb("cb", [128, 256], BF16)
        ident = cb[:, 0:128]
        tri = cb[:, 128:256]
        onesb = sb("onesb", [128, 512], BF16)
        smask = sb("smask", [128, 128], F32)
        r_const = Res("const")
        sl_c = K.slot("sl_c")
        K.dma(sp, cb[:], c_b, sl_c, writes=[r_const])
        K.dma(sp, smask[:], c_smask, sl_c, writes=[r_const])
        small = sb("small", [128, 64 * 4 + 64], F32)
        gcols = sb("gcols", [128, 4 * NKC + 8], F32)
        r_small = Res("small")
        for i, v in enumerate([lq1, lk1, lq2, lk2]):
            K.dma(sp, small[:, i * 64:(i + 1) * 64], v.partition_broadcast(128), sl_c, writes=[r_small])
        for i, v in enumerate([g_mix_pre, g_mem_kv, g_mem_pre, g_ffn_pre]):
            K.dma(sp, gcols[:, i * NKC:(i + 1) * NKC], v, sl_c, writes=[r_small])
        K.dma(sp, gcols[:, 64:65], diff_gain, sl_c, writes=[r_small])
        K.dma(sp, gcols[:, 65:66], sb_gain, sl_c, writes=[r_small])
        c15b = sb("c15b", [128, 8], F32)
        K.dma(sp, c15b[:], rel_bias[15:16, :].partition_broadcast(128), sl_c, writes=[r_small])
        r_const.w = (sl_c.sem, sl_c.cnt)
        r_small.w = (sl_c.sem, sl_c.cnt)
        K.op(dve, lambda: nc.vector.memset(onesb[:], 1.0), writes=[r_const])
        sc0 = 256
        K.op(dve, lambda: nc.vector.tensor_tensor(out=small[:, 0:64], in0=small[:, 0:64], in1=small[:, 64:128], op=ALU.mult),
             reads=[r_small], writes=[r_small])
        K.op(dve, lambda: nc.vector.tensor_tensor(out=small[:, 128:192], in0=small[:, 128:192], in1=small[:, 192:256], op=ALU.mult),
             reads=[r_small], writes=[r_small])
        K.op(dve, lambda: nc.vector.reduce_sum(out=small[:, sc0:sc0 + 1], in_=small[:, 0:64], axis=AX.X), reads=[r_small], writes=[r_small])
        K.op(dve, lambda: nc.vector.reduce_sum(out=small[:, sc0 + 1:sc0 + 2], in_=small[:, 128:192], axis=AX.X), reads=[r_small], writes=[r_small])
        K.op(act, lambda: nc.scalar.activation(out=small[:, sc0 + 2:sc0 + 4], in_=small[:, sc0:sc0 + 2], func=AF.Exp), reads=[r_small], writes=[r_small])
        K.op(dve, lambda: nc.vector.tensor_tensor(out=small[:, sc0 + 4:sc0 + 5], in0=small[:, sc0 + 3:sc0 + 4], in1=small[:, sc0 + 2:sc0 + 3], op=ALU.subtract),
             reads=[r_small], writes=[r_small])
        K.op(dve, lambda: nc.vector.tensor_scalar(out=small[:, sc0 + 4:sc0 + 5], in0=small[:, sc0 + 4:sc0 + 5], scalar1=-LAM_INIT, scalar2=None, op0=ALU.add),
             reads=[r_small], writes=[r_small])
        nlam = small[:, sc0 + 4:sc0 + 5]
        K.op(dve, lambda: nc.vector.tensor_scalar(out=gcols[:, 64:65], in0=gcols[:, 64:65], scalar1=math.sqrt(128.0) * (1.0 - LAM_INIT), scalar2=None, op0=ALU.mult),
             reads=[r_small], writes=[r_small])
        K.op(dve, lambda: nc.vector.tensor_scalar(out=gcols[:, 65:66], in0=gcols[:, 65:66], scalar1=math.sqrt(128.0), scalar2=None, op0=ALU.mult),
             reads=[r_small], writes=[r_small])
        gdiff = gcols[:, 64:65]
        gsb = gcols[:, 65:66]
        stat = sb("stat", [128, 16], F32)
        r_stat = Res("stat")
        K.op(dve, lambda: nc.vector.memset(stat[:], 0.0), writes=[r_stat])
        nbias = sb("nbias", [128, 8], F32)
        r_nbias = Res("nbias")
        EB = sb("EB", [128, 2 * 128 * 8], F32)
        r_EB = Res("EB")

        def wview(name, k0, nk, c0, ncols):
            return wb[name].rearrange("(kc p) n -> p kc n", p=128)[:, k0:k0 + nk, c0:c0 + ncols]

        epsb = sb("epsb", [128, 2], F32)
        K.op(dve, lambda: nc.vector.memset(epsb[:, 0:1], EPS), writes=[r_const])
        K.op(dve, lambda: nc.vector.memset(epsb[:, 1:2], 128.0 * EPS), writes=[r_const])

        def rstd_from_ss(ss_ap, out_ap, n, reads, writes):
            K.op(act, lambda: nc.scalar.activation(out=out_ap, in_=ss_ap, func=AF.Ln, bias=epsb[:, 0:1], scale=1.0 / n),
                 reads=list(reads) + [r_const], writes=writes)
            K.op(act, lambda: nc.scalar.activation(out=out_ap, in_=out_ap, func=AF.Exp, scale=-0.5),
                 reads=writes, writes=writes)

        with ExitStack() as pa:
            sbA = lambda name, shape, dt: pa.enter_context(nc.sbuf_tensor(name, list(shape), dt))
            psA = lambda name, shape, dt: pa.enter_context(nc.psum_tensor(name, list(shape), dt))
            xin = [sbA(f"xin{i}", [128, 4, D], F32) for i in range(2)]
            r_xin = [Res(f"xin{i}") for i in range(2)]
            sl_xin = [K.slot(f"sl_xin{i}") for i in range(2)]
            junk = sbA("junkA", [128, D], BF16)
            r_junk = Res("junk")
            ssA = sbA("ssA", [128, 8], F32)
            r_ssA = [Res("ssA0"), Res("ssA1")]
            hb = [sbA(f"hbA{i}", [128, D], BF16) for i in range(2)]
            r_hb = [Res("hb0"), Res("hb1")]
            hT = [sbA(f"hTA{i}", [128, NKC, TT], BF16) for i in range(2)]
            r_hT = [Res("hT0"), Res("hT1")]
            NWS = 3
            ws = [sbA(f"wsA{i}", [128, NKC, 512], BF16) for i in range(NWS)]
            r_ws = [Res(f"wsA{i}") for i in range(NWS)]
            sl_ws = [K.slot(f"sl_wsA{i}") for i in range(NWS)]
            NST = 4
            stg = [sbA(f"stgA{i}", [128, 512], BF16) for i in range(NST)]
            r_stg = [Res(f"stg{i}") for i in range(NST)]
            sl_stg = [K.slot(f"sl_stgA{i}") for i in range(NST)]
            sqb = [sbA(f"sqbA{i}", [128, 512], BF16) for i in range(2)]
            r_sqb = [Res("sqb0"), Res("sqb1")]
            tmpm = sbA("tmpmA", [128, 2], F32)
            r_tmpm = [Res("tmpm0"), Res("tmpm1")]
            acc = [psA(f"accA{i}", [128, 512], F32) for i in range(3)]
            r_acc = [Res(f"accA{i}") for i in range(3)]
            tp = [psA(f"tpA{i}", [128, 8, 128], BF16) for i in range(2)]
            r_tp = [Res("tp0"), Res("tp1")]
            stp = psA("stpA", [128, 512], F32)
            r_stp = Res("stp")
            r_scr_qk = [[Res(f"qk{fc}_{t}") for t in range(NT)] for fc in range(32)]
            r_scr_v = [Res(f"v_{t}") for t in range(NT)]

            def load_x(t):
                b = t % 2
                K.dma(sp, xin[b][:], x[t * TT:(t + 1) * TT, :].rearrange("(j p) d -> p j d", p=128), sl_xin[b], writes=[r_xin[b]])

            cnt = {"w": 0, "acc": 0, "stg": 0, "sq": 0}
            wq = []

            def issue_w(c):
                i = cnt["w"] % NWS
                cnt["w"] += 1
                K.dma(sp, ws[i][:], wview("w_in", 0, NKC, c * 512, 512), sl_ws[i], reads=[wres["w_in"]], writes=[r_ws[i]])
                return i

            gpreA = sbA("gpreA", [128, D], F32)
            r_gpreA = Res("gpreA")
            sl_gA = K.slot("sl_gA")
            K.dma(sp, gpreA[:], gv_mix_pre.partition_broadcast(128), sl_gA, writes=[r_gpreA])
            r_hTh = [[Res(f"hT{b}_{hf}") for hf in range(2)] for b in range(2)]

            def prologue(t):
                b = t % 2
                for j in range(4):
                    hbj = j % 2
                    K.op(act, lambda: nc.scalar.activation(out=junk[:], in_=xin[b][:, j, :], func=AF.Square, accum_out=ssA[:, j:j + 1]),
                         reads=[r_xin[b]], writes=[r_junk, r_ssA[hbj]])
                    rstd_from_ss(ssA[:, j:j + 1], ssA[:, 4 + j:5 + j], D, [r_ssA[hbj]], [r_ssA[hbj]])
                    K.op(dve, lambda: nc.vector.scalar_tensor_tensor(out=hb[hbj][:], in0=xin[b][:, j, :], scalar=ssA[:, 4 + j:5 + j], in1=gpreA[:],
                                                                    op0=ALU.mult, op1=ALU.mult),
                         reads=[r_xin[b], r_ssA[hbj], r_gpreA], writes=[r_hb[hbj]])
                    for half in range(2):
                        K.mm([(lambda kc=kc: nc.tensor.transpose(out=tp[half][:, kc % 8, :], in_=hb[hbj][:, kc * 128:(kc + 1) * 128], identity=ident))
                              for kc in range(half * 8, half * 8 + 8)],
                             reads=[r_hb[hbj], r_const], writes=[r_tp[half]])
                        if half == 0:
                            K.op(act, lambda: nc.scalar.activation(out=hT[b][:, 0:8, j * 128:(j + 1) * 128], in_=tp[0][:], func=AF.Copy),
                                 reads=[r_tp[0]], writes=[r_hTh[b][0]])
                        else:
                            K.op(dve, lambda: nc.vector.tensor_copy(out=hT[b][:, 8:16, j * 128:(j + 1) * 128], in_=tp[1][:]),
                                 reads=[r_tp[1]], writes=[r_hTh[b][1]])

            load_x(0)
            chunks = [(t, c) for t in range(NT) for c in range(12)]
            pend = []
            PRE = 2
            ci = 0
            for _ in range(PRE):
                if ci < len(chunks):
                    pend.append(issue_w(chunks[ci][1])); ci += 1
            prologue(0)
            for t in range(NT):
                b = t % 2
                if t + 1 < NT:
                    load_x(t + 1)
                for c in range(12):
                    if c == 8 and t + 1 < NT:
                        prologue(t + 1)
                    wi = pend.pop(0)
                    if ci < len(chunks):
                        pend.append(issue_w(chunks[ci][1])); ci += 1
                    grp = c // 2
                    if grp in (2, 5):
                        vcol0 = (0 if grp == 2 else 1024) + (c % 2) * 512
                        for j in range(4):
                            a = cnt["acc"] % 3; cnt["acc"] += 1
                            K.mm([(lambda kc=kc: nc.tensor.matmul(acc[a][:], lhsT=hT[b][:, kc, j * 128:(j + 1) * 128], rhs=ws[wi][:, kc, :],
                                                                  start=(kc == 0), stop=(kc == NKC - 1))) for kc in range(NKC)],
                                 reads=[r_hTh[b][0], r_hTh[b][1], r_ws[wi]], writes=[r_acc[a]])
                            s_ = cnt["stg"] % NST; cnt["stg"] += 1
                            K.op(act, lambda: nc.scalar.activation(out=stg[s_][:], in_=acc[a][:], func=AF.Copy),
                                 reads=[r_acc[a]], writes=[r_stg[s_]])
                            K.dma(sp, vscr[t * TT + j * 128:t * TT + (j + 1) * 128, vcol0:vcol0 + 512], stg[s_][:], sl_stg[s_],
                                  reads=[r_stg[s_]], writes=[r_scr_v[t]])
                    else:
                        fbase = {0: 0, 1: 8, 3: 16, 4: 24}[grp] + (c % 2) * 4
                        qscale = {0: 0.125, 1: 1.0, 3: 128.0 ** -0.5, 4: 1.0}[grp]
                        for fi in range(4):
                            fc = fbase + fi
                            a = cnt["acc"] % 3; cnt["acc"] += 1
                            K.mm([(lambda kc=kc: nc.tensor.matmul(acc[a][:], lhsT=ws[wi][:, kc, fi * 128:(fi + 1) * 128], rhs=hT[b][:, kc, :],
                                                                  start=(kc == 0), stop=(kc == NKC - 1))) for kc in range(NKC)],
                                 reads=[r_hTh[b][0], r_hTh[b][1], r_ws[wi]], writes=[r_acc[a]])
                            s_ = cnt["stg"] % NST; cnt["stg"] += 1
                            K.op(act, lambda: nc.scalar.activation(out=stg[s_][:], in_=acc[a][:], func=AF.Copy, scale=qscale),
                                 reads=[r_acc[a]], writes=[r_stg[s_]])
                            if grp in (0, 1):
                                q_ = cnt["sq"] % 2; cnt["sq"] += 1
                                K.op(act, lambda: nc.scalar.activation(out=sqb[q_][:], in_=acc[a][:], func=AF.Square, scale=qscale),
                                     reads=[r_acc[a]], writes=[r_sqb[q_]])
                                K.mm([lambda: nc.tensor.matmul(stp[:], lhsT=onesb[:, 0:128], rhs=sqb[q_][:], start=True, stop=True)],
                                     reads=[r_sqb[q_], r_const], writes=[r_stp])
                                K.op(dve, lambda: nc.vector.reduce_max(out=tmpm[:, q_:q_ + 1], in_=stp[:], axis=AX.X),
                                     reads=[r_stp], writes=[r_tmpm[q_]])
                                col = fc
                                K.op(dve, lambda: nc.vector.tensor_tensor(out=stat[:, col:col + 1], in0=stat[:, col:col + 1], in1=tmpm[:, q_:q_ + 1], op=ALU.max),
                                     reads=[r_tmpm[q_], r_stat], writes=[r_stat])
                            K.dma(sp, qkT[fc, :, t * TT:(t + 1) * TT], stg[s_][:], sl_stg[s_], reads=[r_stg[s_]], writes=[r_scr_qk[fc][t]])
            K.barrier()

        K.op(dve, lambda: nc.vector.tensor_scalar(out=stat[:, 0:8], in0=stat[:, 0:8], scalar1=-4.2, scalar2=None, op0=ALU.mult),
             reads=[r_stat], writes=[r_stat])
        K.op(dve, lambda: nc.vector.scalar_tensor_tensor(out=nbias[:], in0=stat[:, 8:16], scalar=-1.0 / 15.0, in1=stat[:, 0:8], op0=ALU.mult, op1=ALU.add),
             reads=[r_stat], writes=[r_nbias])
        K.op(dve, lambda: nc.vector.tensor_tensor(out=nbias[:], in0=nbias[:], in1=c15b[:], op=ALU.add),
             reads=[r_small, r_nbias], writes=[r_nbias])

        with ExitStack() as pbias:
            sbB = lambda name, shape, dt: pbias.enter_context(nc.sbuf_tensor(name, list(shape), dt))
            psB = lambda name, shape, dt: pbias.enter_context(nc.psum_tensor(name, list(shape), dt))
            tab = sbB("tab", [33, 8], F32)
            r_tab = Res("tab")
            sl_b = K.slot("sl_b")
            K.op(dve, lambda: nc.vector.memset(tab[:], -30000.0), writes=[r_tab])
            K.dma(sp, tab[0:32, :], rel_bias, sl_b, writes=[r_tab])
            ohb = [sbB(f"ohb{i}", [33, 4096], F32) for i in range(2)]
            r_ohb = [Res("ohb0"), Res("ohb1")]
            sl_oh = [K.slot("sl_oh0"), K.slot("sl_oh1")]
            bps = [psB(f"bps{i}", [128, 512], F32) for i in range(2)]
            r_bps = [Res("bps0"), Res("bps1")]
            btmp = sbB("btmp", [128, 512], F32)
            r_btmp = Res("btmp")
            for pc in range(8):
                i = pc % 2
                K.dma(sp, ohb[i][:], c_oh[:, pc * 4096:(pc + 1) * 4096], sl_oh[i], writes=[r_ohb[i]])
                bi = pc % 2
                K.mm([(lambda qq=qq: nc.tensor.matmul(bps[bi][:, qq * 8:(qq + 1) * 8], lhsT=ohb[i][:, qq * 128:(qq + 1) * 128], rhs=tab[:],
                                                       start=True, stop=True)) for qq in range(32)],
                     reads=[r_ohb[i], r_tab], writes=[r_bps[bi]])
                K.op(dve, lambda: nc.vector.tensor_tensor(out=btmp[:, 0:256].rearrange("p (q h) -> p q h", h=8),
                                                          in0=bps[bi][:, 0:256].rearrange("p (q h) -> p q h", h=8),
                                                          in1=c15b[:].unsqueeze(1).to_broadcast([128, 32, 8]), op=ALU.subtract),
                     reads=[r_bps[bi], r_small], writes=[r_btmp])
                K.op(act, lambda: nc.scalar.activation(out=EB[:, pc * 256:(pc + 1) * 256], in_=btmp[:, 0:256], func=AF.Exp),
                     reads=[r_btmp], writes=[r_EB])
            K.barrier()
        EBv = EB[:].rearrange("p (dl q h) -> p dl q h", dl=2, q=128, h=8)

        with ExitStack() as pb:
            sbB = lambda name, shape, dt: pb.enter_context(nc.sbuf_tensor(name, list(shape), dt))
            psB = lambda name, shape, dt: pb.enter_context(nc.psum_tensor(name, list(shape), dt))
            QT = [sbB(f"QT{i}", [128, S], BF16) for i in range(4)]
            KT = [sbB(f"KT{i}", [128, S], BF16) for i in range(4)]
            VV = [sbB(f"VV{i}", [128, NB, 128], BF16) for i in range(4)]
            r_qkv = [Res(f"qkv{i}") for i in range(4)]
            sl_qkv = [K.slot(f"sl_qkv{i}") for i in range(4)]
            r_all_qk = [r for l in r_scr_qk for r in l]
            r_attn = [[Res(f"attn{h}_{g}") for g in range(NG)] for h in range(16)]
            NFIN = 2
            fin = [sbB(f"fin{i}", [128, 512], BF16) for i in range(NFIN)]
            r_fin = [Res(f"fin{i}") for i in range(NFIN)]
            sl_fin = [K.slot(f"sl_fin{i}") for i in range(NFIN)]
            fcnt = {"fin": 0}
            nrm = psB("nrmB", [128, 512], F32)
            r_nrm = Res("nrm")
            sqB = sbB("sqB", [128, 512], BF16)
            r_sqB = Res("sqB")
            rsB = sbB("rsB", [128, 512], F32)
            r_rsB = Res("rsB")
            osb = [sbB(f"osb{i}", [128, 512], F32) for i in range(2)]
            r_osb = [Res("osb0"), Res("osb1")]

            def load_head(slot_i, qfc, kfc, vcol):
                K.dma(sp, QT[slot_i][:], qkT[qfc], sl_qkv[slot_i], reads=r_all_qk[qfc * NT:(qfc + 1) * NT], writes=[r_qkv[slot_i]])
                K.dma(sp, KT[slot_i][:], qkT[kfc], sl_qkv[slot_i], reads=r_all_qk[kfc * NT:(kfc + 1) * NT], writes=[r_qkv[slot_i]])
                K.dma(sp, VV[slot_i][:], vscr[:, vcol:vcol + 128].rearrange("(kb p) d -> p kb d", p=128), sl_qkv[slot_i],
                      reads=r_scr_v, writes=[r_qkv[slot_i]])

            def head_norm_store(o_ap, r_o, gcol, hidx, g):
                K.op(act, lambda: nc.scalar.activation(out=sqB[:], in_=o_ap, func=AF.Square), reads=[r_o], writes=[r_sqB])
                K.mm([lambda: nc.tensor.matmul(nrm[:], lhsT=onesb[:, 0:128], rhs=sqB[:], start=True, stop=True)],
                     reads=[r_sqB, r_const], writes=[r_nrm])
                K.op(act, lambda: nc.scalar.activation(out=rsB[:], in_=nrm[:], func=AF.Ln, bias=epsb[:, 1:2], scale=1.0),
                     reads=[r_nrm, r_const], writes=[r_rsB])
                K.op(act, lambda: nc.scalar.activation(out=rsB[:], in_=rsB[:], func=AF.Exp, scale=-0.5),
                     reads=[r_rsB], writes=[r_rsB])
                f_ = fcnt["fin"] % NFIN; fcnt["fin"] += 1
                K.op(dve, lambda: nc.vector.scalar_tensor_tensor(out=fin[f_][:], in0=o_ap, scalar=gcol, in1=rsB[:], op0=ALU.mult, op1=ALU.mult),
                     reads=[r_o, r_rsB, r_small], writes=[r_fin[f_]])
                K.dma(sp, attnT[hidx, :, g * 512:(g + 1) * 512], fin[f_][:], sl_fin[f_], reads=[r_fin[f_]], writes=[r_attn[hidx][g]])

            with ExitStack() as pm:
                psM = lambda name, shape, dt: pm.enter_context(nc.psum_tensor(name, list(shape), dt))
                sbM = lambda name, shape, dt: pm.enter_context(nc.sbuf_tensor(name, list(shape), dt))
                Sps = [psM(f"Sps{m}", [128, 512], F32) for m in range(2)]
                r_S = [Res("S0"), Res("S1")]
                num = [psM(f"num{m}", [128, 512], F32) for m in range(2)]
                r_num = [Res("num0"), Res("num1")]
                zb2 = [psM(f"zb2_{i}", [128, 512], F32) for i in range(2)]; r_z = [Res("z0"), Res("z1")]
                aps = psM("aps", [128, 512], F32); r_a = Res("a")
                NE = 3
                Eb = [[sbM(f"E{m}_{i}", [128, 512], BF16) for i in range(NE)] for m in range(2)]
                r_E = [[Res(f"E{m}_{i}") for i in range(NE)] for m in range(2)]
                dacc = [[sbM(f"dacc{m}_{p}", [128, 512], F32) for p in range(2)] for m in range(2)]
                r_dacc = [[Res(f"dacc{m}_{p}") for p in range(2)] for m in range(2)]
                numS = [sbM(f"numS{m}", [128, 512], F32) for m in range(2)]
                r_numS = [Res("numS0"), Res("numS1")]
                post_ops = []
                chain_i = 0
                rr = [sbM(f"rr{m}", [128, 512], F32) for m in range(2)]
                r_rr = [Res("rr0"), Res("rr1")]
                ab = [sbM(f"ab{m}", [128, 512], F32) for m in range(2)]
                r_ab = [Res("ab0"), Res("ab1")]
                eb = [sbM(f"e_{i}", [128, 512], F32) for i in range(2)]
                r_e = [Res("e0"), Res("e1")]
                spb = [sbM(f"sp_{i}", [128, 512], BF16) for i in range(2)]
                r_sp = [Res("sp0"), Res("sp1")]
                ecb = sbM("ecb", [128, 512], F32); r_ec = Res("ec")
                wbf = [sbM(f"w_{i}", [128, 512], BF16) for i in range(2)]
                r_w = [Res("w0"), Res("w1")]
                racc = sbM("racc", [128, 512], BF16); r_racc = Res("racc")
                zb = sbM("zb", [128, 512], BF16)
                onesf = sbM("onesf", [128, 128], F32)
                K.op(dve, lambda: nc.vector.memset(zb[:], 0.0), writes=[r_const])
                ntri = sbM("ntri", [128, 128], BF16)
                nones = sbM("nones", [128, 128], BF16)
                K.op(dve, lambda: nc.vector.memset(nones[:], -1.0), writes=[r_const])
                K.op(dve, lambda: nc.vector.tensor_scalar(out=ntri[:], in0=tri, scalar1=-1.0, scalar2=None, op0=ALU.mult), reads=[r_const], writes=[r_const])
                K.op(dve, lambda: nc.vector.memset(onesf[:], 1.0), writes=[r_const])

                def load_pair(h):
                    load_head((h % 2) * 2, h, 8 + h, h * 128)
                    load_head((h % 2) * 2 + 1, 16 + h, 24 + h, 1024 + h * 128)

                load_pair(0)
                ecnt = 0
                for h in range(8):
                    sd = (h % 2) * 2
                    ss_ = sd + 1
                    if h + 1 < 8:
                        load_pair(h + 1)
                    for g in range(NG):
                        KS = 4 * g + 4
                        dp = chain_i % 2
                        chain_i += 1

                        def q0_of(kb):
                            i = kb - 4 * g
                            return (128 * i if i > 0 else 0), i

                        def d_S(k):
                            q0, _ = q0_of(k)
                            for m in range(2):
                                K.mm([lambda m=m: nc.tensor.matmul(Sps[m][:, q0:512], lhsT=KT[sd][64 * m:64 * m + 64, k * 128:(k + 1) * 128],
                                                                   rhs=QT[sd][64 * m:64 * m + 64, g * 512 + q0:(g + 1) * 512], start=True, stop=True)],
                                     reads=[r_qkv[sd]], writes=[r_S[m]])

                        def d_E(k, e_):
                            q0, _ = q0_of(k)
                            for m in range(2):
                                K.op(act, lambda m=m: nc.scalar.activation(out=Eb[m][e_][:, q0:512], in_=Sps[m][:, q0:512], func=AF.Exp,
                                                                          bias=nbias[:, h:h + 1], scale=1.0),
                                     reads=[r_S[m], r_nbias], writes=[r_E[m][e_]])
                                for jq in range(4):
                                    dl = (4 * g + jq) - k
                                    if dl in (0, 1) and jq * 128 >= q0:
                                        K.op(dve, lambda m=m, jq=jq, dl=dl: nc.vector.tensor_tensor(
                                            out=Eb[m][e_][:, jq * 128:(jq + 1) * 128], in0=Eb[m][e_][:, jq * 128:(jq + 1) * 128],
                                            in1=EBv[:, dl, :, h], op=ALU.mult),
                                            reads=[r_E[m][e_], r_EB], writes=[r_E[m][e_]])

                        def d_PV(k, e_):
                            q0, _ = q0_of(k)
                            for m in range(2):
                                K.mm([lambda m=m: nc.tensor.matmul(num[m][:, q0:512], lhsT=VV[sd][:, k, :], rhs=Eb[m][e_][:, q0:512],
                                                                   start=(k == 0), stop=(k == KS - 1))],
                                     reads=[r_E[m][e_], r_qkv[sd]], writes=[r_num[m]])
                            for m, (eng, ee) in enumerate(((pool, nc.gpsimd), (dve, nc.vector))):
                                if k == 0:
                                    K.op(eng, lambda m=m, ee=ee: ee.tensor_copy(out=dacc[m][dp][:], in_=Eb[m][e_][:]),
                                         reads=[r_E[m][e_]], writes=[r_dacc[m][dp]])
                                else:
                                    K.op(eng, lambda m=m, ee=ee: ee.tensor_tensor(out=dacc[m][dp][:, q0:512], in0=dacc[m][dp][:, q0:512], in1=Eb[m][e_][:, q0:512], op=ALU.add),
                                         reads=[r_E[m][e_], r_dacc[m][dp]], writes=[r_dacc[m][dp]])

                        def s_kb(k):
                            return 4 * g + 3 - k

                        def s_z(k):
                            kb = s_kb(k); q0, i = q0_of(kb); b_ = k % 2
                            K.mm([lambda: nc.tensor.matmul(zb2[b_][:, q0:512], lhsT=KT[ss_][:, kb * 128:(kb + 1) * 128],
                                                           rhs=QT[ss_][:, g * 512 + q0:(g + 1) * 512], start=True, stop=True)],
                                 reads=[r_qkv[ss_]], writes=[r_z[b_]])

                        def s_e(k):
                            kb = s_kb(k); q0, i = q0_of(kb); b_ = k % 2
                            K.op(act, lambda: nc.scalar.activation(out=eb[b_][:, q0:512], in_=zb2[b_][:, q0:512], func=AF.Exp),
                                 reads=[r_z[b_]], writes=[r_e[b_]])
                            if i >= 0:
                                K.op(dve, lambda: nc.vector.tensor_tensor(out=eb[b_][:, q0:q0 + 128], in0=eb[b_][:, q0:q0 + 128], in1=smask[:], op=ALU.mult),
                                     reads=[r_e[b_], r_const], writes=[r_e[b_]])

                        def s_SP(k):
                            kb = s_kb(k); q0, i = q0_of(kb); b_ = k % 2
                            K.op(act, lambda: nc.scalar.activation(out=spb[b_][:, q0:512], in_=eb[b_][:, q0:512], func=AF.Ln, bias=1.0, scale=1.0),
                                 reads=[r_e[b_]], writes=[r_sp[b_]])

                        def s_cum(k):
                            kb = s_kb(k); q0, i = q0_of(kb); b_ = k % 2
                            fns = [lambda: nc.tensor.matmul(zb2[b_][:, q0:512], lhsT=ntri[:], rhs=spb[b_][:, q0:512], start=False, stop=(k == 0),
                                                            skip_group_check=True)]
                            rd = [r_sp[b_], r_const]
                            if k > 0:
                                fns.append(lambda: nc.tensor.matmul(zb2[b_][:, q0:512], lhsT=nones[:], rhs=racc[:, q0:512], start=False, stop=True,
                                                                    skip_group_check=True))
                                rd.append(r_racc)
                            K.mm(fns, reads=rd, writes=[r_z[b_]])
                            if k + 1 < KS:
                                if k == 0:
                                    K.op(dve, lambda: nc.vector.tensor_copy(out=racc[:, q0:512], in_=spb[b_][:, q0:512]),
                                         reads=[r_sp[b_]], writes=[r_racc])
                                else:
                                    K.op(dve, lambda: nc.vector.tensor_tensor(out=racc[:, q0:512], in0=racc[:, q0:512], in1=spb[b_][:, q0:512], op=ALU.add),
                                         reads=[r_sp[b_], r_racc], writes=[r_racc])

                        def s_w(k):
                            kb = s_kb(k); q0, i = q0_of(kb); b_ = k % 2
                            K.op(act, lambda: nc.scalar.activation(out=wbf[b_][:, q0:512], in_=zb2[b_][:, q0:512], func=AF.Exp),
                                 reads=[r_z[b_]], writes=[r_w[b_]])
                            if i >= 0:
                                K.op(dve, lambda: nc.vector.tensor_tensor(out=wbf[b_][:, q0:q0 + 128], in0=wbf[b_][:, q0:q0 + 128], in1=smask[:], op=ALU.mult),
                                     reads=[r_w[b_], r_const], writes=[r_w[b_]])

                        def s_pv(k):
                            kb = s_kb(k); q0, i = q0_of(kb); b_ = k % 2
                            K.mm([lambda: nc.tensor.matmul(aps[:, q0:512], lhsT=VV[ss_][:, kb, :], rhs=wbf[b_][:, q0:512], start=False, stop=(k == KS - 1))],
                                 reads=[r_w[b_], r_qkv[ss_]], writes=[r_a])

                        K.op(dve, lambda: nc.vector.memset(racc[:, 0:384], 0.0), writes=[r_racc])
                        K.mm([lambda: nc.tensor.matmul(aps[:], lhsT=onesb[:, 0:128], rhs=zb[:], start=True, stop=False)],
                             reads=[r_const], writes=[r_a])
                        d_S(0)
                        s_z(0)
                        s_e(0)
                        for k in range(KS):
                            e_ = ecnt % NE; ecnt += 1
                            if k > 0:
                                s_w(k - 1)
                            s_SP(k)
                            s_cum(k)
                            if k + 1 < KS:
                                s_z(k + 1)
                            if k > 0:
                                s_pv(k - 1)
                            d_E(k, e_)
                            if k + 1 < KS:
                                d_S(k + 1)
                            d_PV(k, e_)
                            if k + 1 < KS:
                                s_e(k + 1)
                            if post_ops and k >= 1:
                                post_ops.pop(0)()
                        s_w(KS - 1)
                        s_pv(KS - 1)
                        while post_ops:
                            post_ops.pop(0)()
                        for m in range(2):
                            K.op(dve, lambda m=m: nc.vector.tensor_copy(out=numS[m][:], in_=num[m][:]), reads=[r_num[m]], writes=[r_numS[m]])
                        K.op(dve, lambda: nc.vector.tensor_copy(out=osb[1][:], in_=aps[:]), reads=[r_a], writes=[r_osb[1]])

                        def mk_post(dp=dp, h=h, g=g):
                            def den_rr(m):
                                K.mm([lambda: nc.tensor.matmul(nrm[:], lhsT=onesf[:], rhs=dacc[m][dp][:], start=True, stop=True)],
                                     reads=[r_dacc[m][dp], r_const], writes=[r_nrm])
                                K.op(dve, lambda: nc.vector.reciprocal(out=rr[m][:], in_=nrm[:]), reads=[r_nrm], writes=[r_rr[m]])
                                K.op(dve, lambda: nc.vector.tensor_tensor(out=ab[m][:], in0=numS[m][:], in1=rr[m][:], op=ALU.mult),
                                     reads=[r_numS[m], r_rr[m]], writes=[r_ab[m]])

                            def comb():
                                K.op(dve, lambda: nc.vector.scalar_tensor_tensor(out=osb[0][:], in0=ab[1][:], scalar=nlam, in1=ab[0][:], op0=ALU.mult, op1=ALU.add),
                                     reads=[r_ab[0], r_ab[1], r_small], writes=[r_osb[0]])
                                head_norm_store(osb[0][:], r_osb[0], gdiff, h, g)

                            return [lambda: den_rr(0), lambda: den_rr(1), comb,
                                    lambda: head_norm_store(osb[1][:], r_osb[1], gsb, 8 + h, g)]

                        post_ops.extend(mk_post())
                while post_ops:
                    post_ops.pop(0)()
                K.barrier()

        with ExitStack() as pc_:
            sbC = lambda name, shape, dt: pc_.enter_context(nc.sbuf_tensor(name, list(shape), dt))
            psC = lambda name, shape, dt: pc_.enter_context(nc.psum_tensor(name, list(shape), dt))
            xres = sbC("xres", [128, 4, D], F32); r_x = Res("xres"); sl_x = K.slot("sl_x")
            mix = sbC("mix", [128, 4, D], F32); r_mix = [Res(f"mix{j}") for j in range(4)]
            hbC = sbC("hbC", [128, D], BF16); r_hbC = Res("hbC")
            junkC = sbC("junkC", [128, D], BF16); r_junkC = Res("junkC")
            fmA = sbC("fmA", [128, NKC, TT], BF16); r_fmA = Res("fmA"); sl_fmA = K.slot("sl_fmA")
            RR = sbC("RR", [128, 24 * 512], BF16); r_RR = Res("RR")
            fmB = RR[:, 0:NKC * 512].rearrange("p (k t) -> p k t", t=512)
            PT = RR[:, NKC * 512:NKC * 512 + 4096].rearrange("p (h m t) -> p h m t", h=4, m=2)
            actT = RR[:].rearrange("p (k t) -> p k t", t=512)
            KmT = sbC("KmT", [128, NKC, NMEM], BF16); r_Km = Res("KmT")
            Vm = sbC("Vm", [128, 2, D], BF16); r_Vm = Res("Vm")
            gpost = sbC("gpost", [128, D], F32); r_gpost = Res("gpost"); sl_g = K.slot("sl_g")
            NWS = 3
            ws = [sbC(f"wsC{i}", [128, NKC, 512], BF16) for i in range(NWS)]
            r_ws = [Res(f"wsC{i}") for i in range(NWS)]
            sl_ws = [K.slot(f"sl_wsC{i}") for i in range(NWS)]
            ssC = sbC("ssC", [128, 16], F32); r_ssC = Res("ssC")
            smx = sbC("smx", [128, 16], F32); r_smx = [Res("smx0"), Res("smx1")]
            Pf = [sbC(f"Pf{i}", [128, NMEM], F32) for i in range(2)]; r_Pf = [Res("Pf0"), Res("Pf1")]
            Pn = [sbC(f"Pn{i}", [128, NMEM], BF16) for i in range(2)]; r_Pn = [Res("Pn0"), Res("Pn1")]
            sg = [sbC(f"sg{i}", [128, 512], F32) for i in range(2)]; r_sg = [Res("sg0"), Res("sg1")]
            NACC = 4
            acc = [psC(f"accC{i}", [128, 512], F32) for i in range(NACC)]
            r_acc = [Res(f"accC{i}") for i in range(NACC)]
            tpC = psC("tpC", [128, 8, 128], BF16); r_tp = Res("tpC")
            scp_ = [psC(f"scp{i}", [128, 512], F32) for i in range(2)]; r_sc = [Res("sc0"), Res("sc1")]
            ptp_ = psC("ptp", [128, 8, 128], BF16); r_ptp1 = Res("ptp"); r_ptp = [r_ptp1, r_ptp1]
            sl_y = K.slot("sl_y")
            r_y = Res("y")
            cnt = {"w": 0, "acc": 0}

            wplan = []

            def plan_tile():
                L = []
                for c in range(4): L.append([("w_out", 0, NKC, c * 512, 512, 0)])
                for c in range(4): L.append([("w_mq", 0, NKC, c * 512, 512, 0)])
                for c in range(4): L.append([("w_mo", 0, NKC, c * 512, 512, 0)])
                for hf, (c0, c1) in enumerate(((0, 12), (12, 22))):
                    for c2 in range(c0, c1):
                        L.append([("w_gu", 0, NKC, c2 * 256, 256, 0), ("w_gu", 0, NKC, DFF + c2 * 256, 256, 256)])
                    f0, f1 = c0 * 2, c1 * 2
                    nf = (f1 - f0) // 2
                    for c in range(4):
                        L.append([("w_down", f0, nf, c * 512, 512, 0)])
                        L.append([("w_down", f0 + nf, nf, c * 512, 512, 0)])
                return L

            for c in range(8): wplan.append([("w_mkv", 0, NKC, c * 512, 512, 0)])
            for t in range(NT): wplan.extend(plan_tile())
            wpos = {"i": 0}
            pend = []

            def issue_w():
                if wpos["i"] >= len(wplan):
                    return
                specs = wplan[wpos["i"]]; wpos["i"] += 1
                i = cnt["w"] % NWS; cnt["w"] += 1
                for (name, k0, nk, c0, ncols, d0) in specs:
                    K.dma(sp, ws[i][:, 0:nk, d0:d0 + ncols], wview(name, k0, nk, c0, ncols), sl_ws[i], reads=[wres[name]], writes=[r_ws[i]])
                pend.append(i)

            def next_w():
                wi = pend.pop(0)
                issue_w()
                return wi

            def nacc():
                a = cnt["acc"] % NACC; cnt["acc"] += 1
                return a

            for _ in range(2): issue_w()

            def prenorm_T(src_tile, nsub, r_src, gvec, dstT, r_dst):
                K.dma(pool, gpost[:], gvec.partition_broadcast(128), sl_g, writes=[r_gpost])
                for j in range(nsub):
                    K.op(act, lambda: nc.scalar.activation(out=junkC[:], in_=src_tile[:, j, :], func=AF.Square, accum_out=ssC[:, j:j + 1]),
                         reads=[r_src], writes=[r_junkC, r_ssC])
                    rstd_from_ss(ssC[:, j:j + 1], ssC[:, 4 + j:5 + j], D, [r_ssC], [r_ssC])
                    K.op(dve, lambda: nc.vector.scalar_tensor_tensor(out=hbC[:], in0=src_tile[:, j, :], scalar=ssC[:, 4 + j:5 + j], in1=gpost[:],
                                                                    op0=ALU.mult, op1=ALU.mult),
                         reads=[r_src, r_ssC, r_gpost], writes=[r_hbC])
                    for half in range(2):
                        K.mm([(lambda kc=kc: nc.tensor.transpose(out=tpC[:, kc % 8, :], in_=hbC[:, kc * 128:(kc + 1) * 128], identity=ident))
                              for kc in range(half * 8, half * 8 + 8)], reads=[r_hbC, r_const], writes=[r_tp])
                        if half == 0:
                            K.op(act, lambda: nc.scalar.activation(out=dstT[:, 0:8, j * 128:(j + 1) * 128], in_=tpC[:], func=AF.Copy),
                                 reads=[r_tp], writes=[r_dst])
                        else:
                            K.op(dve, lambda: nc.vector.tensor_copy(out=dstT[:, 8:16, j * 128:(j + 1) * 128], in_=tpC[:]),
                                 reads=[r_tp], writes=[r_dst])

            memx = mix[:, 0:2, :]
            sl_m = K.slot("sl_m")
            K.dma(pool, memx, mem.rearrange("(j p) d -> p j d", p=128), sl_m, writes=[r_mix[0]])
            prenorm_T(memx, 2, r_mix[0], gv_mem_kv, fmA, r_fmA)
            memT = fmA
            for c in range(8):
                wi = next_w()
                if c < 4:
                    for fi in range(4):
                        a = nacc()
                        K.mm([(lambda kc=kc: nc.tensor.matmul(acc[a][:, 0:NMEM], lhsT=ws[wi][:, kc, fi * 128:(fi + 1) * 128], rhs=memT[:, kc, 0:NMEM],
                                                              start=(kc == 0), stop=(kc == NKC - 1))) for kc in range(NKC)],
                             reads=[r_fmA, r_ws[wi]], writes=[r_acc[a]])
                        K.op(act, lambda: nc.scalar.activation(out=KmT[:, c * 4 + fi, :], in_=acc[a][:, 0:NMEM], func=AF.Copy, scale=512.0 ** -0.5),
                             reads=[r_acc[a]], writes=[r_Km])
                else:
                    for mc in range(2):
                        a = nacc()
                        K.mm([(lambda kc=kc: nc.tensor.matmul(acc[a][:], lhsT=memT[:, kc, mc * 128:(mc + 1) * 128], rhs=ws[wi][:, kc, :],
                                                              start=(kc == 0), stop=(kc == NKC - 1))) for kc in range(NKC)],
                             reads=[r_fmA, r_ws[wi]], writes=[r_acc[a]])
                        K.op(act, lambda: nc.scalar.activation(out=Vm[:, mc, (c - 4) * 512:(c - 3) * 512], in_=acc[a][:], func=AF.Copy),
                             reads=[r_acc[a]], writes=[r_Vm])

            def proj_tokmajor(srcT, r_srcT, extra_reads=()):
                for c in range(4):
                    wi = next_w()
                    for j in range(4):
                        a = nacc()
                        K.mm([(lambda kc=kc: nc.tensor.matmul(acc[a][:], lhsT=srcT[:, kc, j * 128:(j + 1) * 128], rhs=ws[wi][:, kc, :],
                                                              start=(kc == 0), stop=(kc == NKC - 1))) for kc in range(NKC)],
                             reads=[r_srcT, r_ws[wi]] + list(extra_reads), writes=[r_acc[a]])
                        if j % 2 == 0:
                            K.op(act, lambda: nc.scalar.activation(out=mix[:, j, c * 512:(c + 1) * 512], in_=acc[a][:], func=AF.Copy),
                                 reads=[r_acc[a]], writes=[r_mix[j]])
                        else:
                            K.op(dve, lambda: nc.vector.tensor_copy(out=mix[:, j, c * 512:(c + 1) * 512], in_=acc[a][:]),
                                 reads=[r_acc[a]], writes=[r_mix[j]])

            def postnorm_residual(gvec):
                K.dma(pool, gpost[:], gvec.partition_broadcast(128), sl_g, writes=[r_gpost])
                for j in range(4):
                    K.op(act, lambda: nc.scalar.activation(out=junkC[:], in_=mix[:, j, :], func=AF.Square, accum_out=ssC[:, 8 + j:9 + j]),
                         reads=[r_mix[j]], writes=[r_junkC, r_ssC])
                    rstd_from_ss(ssC[:, 8 + j:9 + j], ssC[:, 12 + j:13 + j], D, [r_ssC], [r_ssC])
                    K.op(dve, lambda: nc.vector.scalar_tensor_tensor(out=mix[:, j, :], in0=mix[:, j, :], scalar=ssC[:, 12 + j:13 + j], in1=gpost[:], op0=ALU.mult, op1=ALU.mult),
                         reads=[r_mix[j], r_ssC, r_gpost], writes=[r_mix[j]])
                    K.op(pool, lambda: nc.gpsimd.tensor_tensor(out=xres[:, j, :], in0=xres[:, j, :], in1=mix[:, j, :], op=ALU.add),
                         reads=[r_mix[j], r_x], writes=[r_x])

            r_attn_all = [r for l in r_attn for r in l]
            for t in range(NT):
                tok = slice(t * TT, (t + 1) * TT)
                K.dma(pool, xres[:], x[tok, :].rearrange("(j p) d -> p j d", p=128), sl_x, reads=[r_y], writes=[r_x])
                K.dma(sp, fmA[:], attnT[:, :, tok].rearrange("k p t -> p k t"), sl_fmA, reads=r_attn_all if t == 0 else [], writes=[r_fmA])
                proj_tokmajor(fmA, r_fmA)
                postnorm_residual(g_mix_post)
                if DEBUG:
                    K.dma(pool, dbg1[tok, :].rearrange("(j p) d -> p j d", p=128), xres[:], sl_y, reads=[r_x], writes=[r_y])
                prenorm_T(xres, 4, r_x, gv_mem_pre, fmB, r_RR)
                for c in range(4):
                    wi = next_w()
                    for fi in range(4):
                        a = nacc()
                        K.mm([(lambda kc=kc: nc.tensor.matmul(acc[a][:], lhsT=ws[wi][:, kc, fi * 128:(fi + 1) * 128], rhs=fmB[:, kc, :],
                                                              start=(kc == 0), stop=(kc == NKC - 1))) for kc in range(NKC)],
                             reads=[r_RR, r_ws[wi]], writes=[r_acc[a]])
                        K.op(act, lambda: nc.scalar.activation(out=fmA[:, c * 4 + fi, :], in_=acc[a][:], func=AF.Copy),
                             reads=[r_acc[a]], writes=[r_fmA])
                qT = fmA
                for j in range(4):
                    for hd in range(4):
                        u = hd % 2
                        K.mm([(lambda dc=dc: nc.tensor.matmul(scp_[u][:, 0:NMEM], lhsT=qT[:, 4 * hd + dc, j * 128:(j + 1) * 128], rhs=KmT[:, 4 * hd + dc, :],
                                                              start=(dc == 0), stop=(dc == 3))) for dc in range(4)],
                             reads=[r_fmA, r_Km], writes=[r_sc[u]])
                        K.op(dve, lambda: nc.vector.reduce_max(out=smx[:, 4 * u:4 * u + 1], in_=scp_[u][:, 0:NMEM], axis=AX.X), reads=[r_sc[u]], writes=[r_smx[u]])
                        K.op(dve, lambda: nc.vector.tensor_scalar(out=smx[:, 4 * u + 1:4 * u + 2], in0=smx[:, 4 * u:4 * u + 1], scalar1=-1.0, scalar2=None, op0=ALU.mult),
                             reads=[r_smx[u]], writes=[r_smx[u]])
                        K.op(act, lambda: nc.scalar.activation(out=Pf[u][:], in_=scp_[u][:, 0:NMEM], func=AF.Exp, bias=smx[:, 4 * u + 1:4 * u + 2], scale=1.0,
                                                               accum_out=smx[:, 4 * u + 2:4 * u + 3]),
                             reads=[r_sc[u], r_smx[u]], writes=[r_Pf[u], r_smx[u]])
                        K.op(dve, lambda: nc.vector.reciprocal(out=smx[:, 4 * u + 3:4 * u + 4], in_=smx[:, 4 * u + 2:4 * u + 3]), reads=[r_smx[u]], writes=[r_smx[u]])
                        K.op(dve, lambda: nc.vector.tensor_scalar(out=Pn[u][:], in0=Pf[u][:], scalar1=smx[:, 4 * u + 3:4 * u + 4], scalar2=None, op0=ALU.mult),
                             reads=[r_Pf[u], r_smx[u]], writes=[r_Pn[u]])
                        K.mm([(lambda mc=mc: nc.tensor.transpose(out=ptp_[:, mc, :], in_=Pn[u][:, mc * 128:(mc + 1) * 128], identity=ident)) for mc in range(2)],
                             reads=[r_Pn[u], r_const], writes=[r_ptp[u]])
                        K.op(act, lambda: nc.scalar.activation(out=PT[:, hd, :, j * 128:(j + 1) * 128], in_=ptp_[:, 0:2, :], func=AF.Copy),
                             reads=[r_ptp[u]], writes=[r_RR])
                for hd in range(4):
                    for dc in range(4):
                        a = nacc()
                        K.mm([(lambda mc=mc: nc.tensor.matmul(acc[a][:], lhsT=Vm[:, mc, hd * 512 + dc * 128:hd * 512 + (dc + 1) * 128], rhs=PT[:, hd, mc, :],
                                                              start=(mc == 0), stop=(mc == 1))) for mc in range(2)],
                             reads=[r_RR, r_Vm], writes=[r_acc[a]])
                        K.op(act, lambda: nc.scalar.activation(out=fmB[:, 4 * hd + dc, :], in_=acc[a][:], func=AF.Copy),
                             reads=[r_acc[a]], writes=[r_RR])
                proj_tokmajor(fmB, r_RR)
                postnorm_residual(g_mem_post)
                if DEBUG:
                    K.dma(pool, dbg2[tok, :].rearrange("(j p) d -> p j d", p=128), xres[:], sl_y, reads=[r_x], writes=[r_y])
                prenorm_T(xres, 4, r_x, gv_ffn_pre, fmA, r_fmA)
                h3T = fmA
                for hf, (c0, c1) in enumerate(((0, 12), (12, 22))):
                    for c2 in range(c0, c1):
                        wi = next_w()
                        for fi in range(2):
                            ag = nacc()
                            K.mm([(lambda kc=kc: nc.tensor.matmul(acc[ag][:], lhsT=ws[wi][:, kc, fi * 128:(fi + 1) * 128], rhs=h3T[:, kc, :],
                                                                  start=(kc == 0), stop=(kc == NKC - 1))) for kc in range(NKC)],
                                 reads=[r_fmA, r_ws[wi]], writes=[r_acc[ag]])
                            au = nacc()
                            K.mm([(lambda kc=kc: nc.tensor.matmul(acc[au][:], lhsT=ws[wi][:, kc, 256 + fi * 128:256 + (fi + 1) * 128], rhs=h3T[:, kc, :],
                                                                  start=(kc == 0), stop=(kc == NKC - 1))) for kc in range(NKC)],
                                 reads=[r_fmA, r_ws[wi]], writes=[r_acc[au]])
                            s_ = fi % 2
                            K.op(act, lambda: nc.scalar.activation(out=sg[s_][:], in_=acc[ag][:], func=AF.Silu), reads=[r_acc[ag]], writes=[r_sg[s_]])
                            K.op(dve, lambda: nc.vector.tensor_tensor(out=actT[:, (c2 - c0) * 2 + fi, :], in0=sg[s_][:], in1=acc[au][:], op=ALU.mult),
                                 reads=[r_sg[s_], r_acc[au]], writes=[r_RR])
                    f0, f1 = c0 * 2, c1 * 2
                    nf = (f1 - f0) // 2
                    for c in range(4):
                        aj = [nacc() for _ in range(4)]
                        for pi in range(2):
                            wi = next_w()
                            for j in range(4):
                                a = aj[j]
                                K.mm([(lambda kk=kk: nc.tensor.matmul(acc[a][:], lhsT=actT[:, pi * nf + kk, j * 128:(j + 1) * 128], rhs=ws[wi][:, kk, :],
                                                                      start=(pi == 0 and kk == 0), stop=(pi == 1 and kk == nf - 1))) for kk in range(nf)],
                                     reads=[r_RR, r_ws[wi]], writes=[r_acc[a]])
                        for j in range(4):
                            a = aj[j]
                            if hf == 0:
                                K.op(act, lambda: nc.scalar.activation(out=mix[:, j, c * 512:(c + 1) * 512], in_=acc[a][:], func=AF.Copy),
                                     reads=[r_acc[a]], writes=[r_mix[j]])
                            else:
                                K.op(dve, lambda: nc.vector.tensor_tensor(out=mix[:, j, c * 512:(c + 1) * 512], in0=mix[:, j, c * 512:(c + 1) * 512], in1=acc[a][:], op=ALU.add),
                                     reads=[r_acc[a], r_mix[j]], writes=[r_mix[j]])
                postnorm_residual(g_ffn_post)
                K.dma(pool, y[tok, :].rearrange("(j p) d -> p j d", p=128), xres[:], sl_y, reads=[r_x], writes=[r_y])
            K.barrier()
    return nc


_CACHE = {}


def _prep_shared(inputs):
    cbm, smask, oh = _consts()
    f = lambda a: np.ascontiguousarray(np.asarray(a, dtype=np.float32))

    def gcol(v):
        return np.ascontiguousarray(f(v).reshape(NKC, 128).T)

    sh = {
        "w_in": f(inputs["w_in"])[0], "w_out": f(inputs["w_out"])[0], "rel_bias": f(inputs["rel_bias"]),
        "lambda_q1": f(inputs["lambda_q1"]), "lambda_k1": f(inputs["lambda_k1"]),
        "lambda_q2": f(inputs["lambda_q2"]), "lambda_k2": f(inputs["lambda_k2"]),
        "diff_sub_gain": f(inputs["diff_sub_gain"]).reshape(128, 1), "sb_gain": f(inputs["sb_gain"]).reshape(128, 1),
        "g_mix_pre": gcol(inputs["g_mix_pre"]), "g_mix_post": f(inputs["g_mix_post"]),
        "w_mq": f(inputs["w_mq"])[0], "w_mkv": f(inputs["w_mkv"])[0], "w_mo": f(inputs["w_mo"])[0],
        "g_mem_kv": gcol(inputs["g_mem_kv"]), "g_mem_pre": gcol(inputs["g_mem_pre"]), "g_mem_post": f(inputs["g_mem_post"]),
        "w_gate_up": f(inputs["w_gate_up"])[0], "w_down": f(inputs["w_down"])[0],
        "g_ffn_pre": gcol(inputs["g_ffn_pre"]), "g_ffn_post": f(inputs["g_ffn_post"]),
        "gv_mix_pre": f(inputs["g_mix_pre"]), "gv_mem_kv": f(inputs["g_mem_kv"]),
        "gv_mem_pre": f(inputs["g_mem_pre"]), "gv_ffn_pre": f(inputs["g_ffn_pre"]),
        "c_b": cbm, "c_smask": smask, "c_oh": oh,
    }
    return sh


def kernel(**inputs):
    x = np.asarray(inputs["x"], dtype=np.float32)
    mem = np.asarray(inputs["mem"], dtype=np.float32)
    B, S, _ = x.shape
    if S not in _CACHE:
        _CACHE[S] = build_nc(S)
    nc = _CACHE[S]
    sh = _prep_shared(inputs)
    in_maps = []
    for b in range(B):
        m = dict(sh)
        m["x"] = np.ascontiguousarray(x[b])
        m["mem"] = np.ascontiguousarray(mem[b])
        in_maps.append(m)
    res = run_bass_kernel_spmd(nc, in_maps, core_ids=list(range(B)))
    if DEBUG:
        for k in ("qkT", "vscr", "attnT", "dbg1", "dbg2"):
            if k in res.results[0]:
                DBG[k] = np.asarray(res.results[0][k])
    return np.stack([np.asarray(r["y"], dtype=np.float32) for r in res.results], axis=0)
```
